# Optimizing a Trainium2 kernel written in Bass

```python
import jax
import jax.numpy as jnp
from jax import lax
import numpy as np

D_MODEL = 2048
BATCH = 1
SEQ = 16384
DEPTH = 4

GRID_W = 64
CTX_LEN = 256
N_MOD = 9
D_FF = 5504
DN_ALPHA = (2 * DEPTH) ** 0.25
DN_BETA = (8 * DEPTH) ** -0.25
LN_EPS = 1e-5
RMS_EPS = 1e-6
CHUNK = 64
ROPE_BASE = 10000.0

GLA_HEADS = 4
GLA_DK = 128
GLA_DV = 256
GLA_GATE_RANK = 16
GLA_GATE_NORM = 16.0
RET_HEADS = 4
RET_DK = 128
RET_DV = 256
GLA_QK = GLA_HEADS * GLA_DK
GLA_V = GLA_HEADS * GLA_DV
RET_QK = RET_HEADS * RET_DK
RET_V = RET_HEADS * RET_DV
EVEN_SPLITS = (GLA_QK, GLA_QK, GLA_V, GLA_V, GLA_GATE_RANK, GLA_GATE_RANK, RET_QK, RET_QK, RET_V, RET_V)
EVEN_IN = 2 * GLA_QK + 2 * GLA_V + 2 * GLA_GATE_RANK + 2 * RET_QK + 2 * RET_V
EVEN_MIX = GLA_V + RET_V

MLA_HEADS = 16
MLA_Q_RANK = 512
MLA_KV_RANK = 512
MLA_NOPE = 128
MLA_ROPE = 64
MLA_DV = 128
MLA_IN = MLA_Q_RANK + MLA_KV_RANK + MLA_ROPE
MLA_SCALE = (MLA_NOPE + MLA_ROPE) ** -0.5
MLA_MIX = MLA_HEADS * MLA_DV
Q_BLOCK = 128

kernel_name = 'hybrid_gla_retnet_mla_flow_trunk'


def layer_norm(x, g, b):
    xf = x.astype(jnp.float32)
    mu = jnp.mean(xf, axis=-1, keepdims=True)
    var = jnp.mean(jnp.square(xf - mu), axis=-1, keepdims=True)
    y = (xf - mu) * lax.rsqrt(var + LN_EPS)
    return (y * g + b).astype(x.dtype)


def rms_norm(x, w=None):
    xf = x.astype(jnp.float32)
    y = xf * lax.rsqrt(jnp.mean(jnp.square(xf), axis=-1, keepdims=True) + RMS_EPS)
    if w is not None:
        y = y * w
    return y.astype(x.dtype)


def modulate(x, shift, scale):
    return x * (1.0 + scale) + shift


def swiglu(h, w_in, w_out):
    gate, up = jnp.split(h @ w_in, 2, axis=-1)
    return (jax.nn.silu(gate) * up) @ w_out


def ffn_sublayer(x, shift, scale, gate, w_in, w_out, g, b):
    y = swiglu(modulate(x, shift, scale), w_in, w_out)
    return layer_norm(DN_ALPHA * x + 0.5 * gate * y, g, b)


def to_heads(a, n_heads):
    B, T, _ = a.shape
    return a.reshape(B, T, n_heads, -1).transpose(0, 2, 1, 3)


def from_heads(a):
    B, H, T, d = a.shape
    return a.transpose(0, 2, 1, 3).reshape(B, T, H * d)


def rope_tables(pos, dim):
    inv = ROPE_BASE ** (-jnp.arange(0, dim, 2, dtype=jnp.float32) / dim)
    ang = pos.astype(jnp.float32)[:, None] * inv[None, :]
    ang = jnp.concatenate([ang, ang], axis=-1)
    return jnp.cos(ang), jnp.sin(ang)


def apply_rope(x, cos, sin):
    x1, x2 = jnp.split(x, 2, axis=-1)
    rot = jnp.concatenate([-x2, x1], axis=-1)
    return x * cos.astype(x.dtype) + rot * sin.astype(x.dtype)


def axial_tables(n_tokens, dim):
    rows = n_tokens // GRID_W
    row = jnp.repeat(jnp.arange(rows), GRID_W)
    col = jnp.tile(jnp.arange(GRID_W), rows)
    return rope_tables(row, dim // 2), rope_tables(col, dim // 2)


def apply_axial(x, tables):
    (cos_r, sin_r), (cos_c, sin_c) = tables
    x_r, x_c = jnp.split(x, 2, axis=-1)
    return jnp.concatenate([apply_rope(x_r, cos_r, sin_r), apply_rope(x_c, cos_c, sin_c)], axis=-1)


def chunks(a):
    B, H, T, d = a.shape
    return jnp.moveaxis(a.reshape(B, H, T // CHUNK, CHUNK, d), 2, 0)


def unchunk(a):
    n, B, H, C, d = a.shape
    return jnp.moveaxis(a, 0, 2).reshape(B, H, n * C, d)


def gla_scan(q, k, v, log_g, s0):
    causal = jnp.tril(jnp.ones((CHUNK, CHUNK), dtype=bool))[:, :, None]

    def step(S, inp):
        qc, kc, vc, gc = (a.astype(jnp.float32) for a in inp)
        b = jnp.cumsum(gc, axis=2)
        rel = jnp.exp(jnp.where(causal, b[:, :, :, None, :] - b[:, :, None, :, :], -jnp.inf))
        A = jnp.einsum('bhid,bhjd,bhijd->bhij', qc, kc, rel)
        o = jnp.einsum('bhij,bhje->bhie', A, vc) + jnp.einsum('bhid,bhde->bhie', qc * jnp.exp(b), S)
        b_end = b[:, :, -1:, :]
        S = jnp.exp(b_end)[:, :, 0, :, None] * S + jnp.einsum('bhjd,bhje->bhde', kc * jnp.exp(b_end - b), vc)
        return S, o

    S, o = lax.scan(step, s0, (chunks(q), chunks(k), chunks(v), chunks(log_g)))
    return unchunk(o), S


def retention_scan(q, k, v, s0, log_gamma):
    idx = jnp.arange(CHUNK, dtype=jnp.float32)
    causal = jnp.tril(jnp.ones((CHUNK, CHUNK), dtype=bool))
    lg = log_gamma[:, None, None]
    dmat = jnp.exp(jnp.where(causal, lg * (idx[:, None] - idx[None, :]), -jnp.inf))
    q_dec = jnp.exp(log_gamma[:, None] * (idx + 1.0))[:, :, None]
    k_dec = jnp.exp(log_gamma[:, None] * (CHUNK - 1.0 - idx))[:, :, None]
    s_dec = jnp.exp(log_gamma * CHUNK)[:, None, None]

    def step(S, inp):
        qc, kc, vc = (a.astype(jnp.float32) for a in inp)
        A = jnp.einsum('bhid,bhjd->bhij', qc, kc) * dmat
        o = jnp.einsum('bhij,bhje->bhie', A, vc) + jnp.einsum('bhid,bhde->bhie', qc * q_dec, S)
        S = s_dec * S + jnp.einsum('bhjd,bhje->bhde', kc * k_dec, vc)
        return S, o

    S, o = lax.scan(step, s0, (chunks(q), chunks(k), chunks(v)))
    return unchunk(o), S


def flip_t(a):
    return jnp.flip(a, axis=2)


def bidir(run_f, run_b, ctx_f, lat_f, ctx_b, lat_b, s0):
    o_cf, s_f = run_f(ctx_f, s0)
    o_lf, _ = run_f(lat_f, s_f)
    o_cb, s_b = run_b(tuple(flip_t(a) for a in ctx_b), s0)
    o_lb, _ = run_b(tuple(flip_t(a) for a in lat_b), s_b)
    return o_cf + flip_t(o_cb), o_lf + flip_t(o_lb)


def even_project(h, w_in, wg_f, bg_f, wg_b, bg_b, ret_rope):
    offs = [int(o) for o in np.cumsum(EVEN_SPLITS)[:-1]]
    gq, gk, gv, gr, gdf, gdb, rq, rk, rv, rg = jnp.split(h @ w_in, offs, axis=-1)
    gla_q = to_heads(gq, GLA_HEADS) * (GLA_DK ** -0.5)
    gla_k = to_heads(gk, GLA_HEADS)
    gla_v = to_heads(gv, GLA_HEADS)
    lg_f = to_heads(jax.nn.log_sigmoid((gdf @ wg_f + bg_f).astype(jnp.float32)) / GLA_GATE_NORM, GLA_HEADS)
    lg_b = to_heads(jax.nn.log_sigmoid((gdb @ wg_b + bg_b).astype(jnp.float32)) / GLA_GATE_NORM, GLA_HEADS)
    ret_q = to_heads(rq, RET_HEADS)
    ret_k = to_heads(rk, RET_HEADS) * (RET_DK ** -0.5)
    if ret_rope is not None:
        ret_q = apply_rope(ret_q, *ret_rope)
        ret_k = apply_rope(ret_k, *ret_rope)
    ret_v = to_heads(rv, RET_HEADS)
    return gla_q, gla_k, gla_v, lg_f, lg_b, gr, ret_q, ret_k, ret_v, rg


def even_output(gla_o, gla_r, ret_o, ret_g, norm_w, w_out):
    dt = gla_r.dtype
    ga = from_heads(rms_norm(gla_o, norm_w)).astype(dt) * jax.nn.silu(gla_r)
    ra = from_heads(rms_norm(ret_o)).astype(dt) * jax.nn.silu(ret_g)
    return jnp.concatenate([ga, ra], axis=-1) @ w_out


def even_mixer(h_ctx, h_lat, ret_rope, w_in, wg_f, bg_f, wg_b, bg_b, norm_w, w_out, need_ctx):
    c_gq, c_gk, c_gv, c_lf, c_lb, c_gr, c_rq, c_rk, c_rv, c_rg = even_project(h_ctx, w_in, wg_f, bg_f, wg_b, bg_b, None)
    l_gq, l_gk, l_gv, l_lf, l_lb, l_gr, l_rq, l_rk, l_rv, l_rg = even_project(h_lat, w_in, wg_f, bg_f, wg_b, bg_b, ret_rope)
    B = h_lat.shape[0]
    gla = lambda t, s: gla_scan(*t, s)
    g_ctx, g_lat = bidir(gla, gla,
                         (c_gq, c_gk, c_gv, c_lf), (l_gq, l_gk, l_gv, l_lf),
                         (c_gq, c_gk, c_gv, c_lb), (l_gq, l_gk, l_gv, l_lb),
                         jnp.zeros((B, GLA_HEADS, GLA_DK, GLA_DV), jnp.float32))
    scales = jnp.arange(RET_HEADS, dtype=jnp.float32)
    lg_fwd = jnp.log1p(-jnp.exp2(-5.0 - scales))
    lg_bwd = jnp.log1p(-jnp.exp2(-5.5 - scales))
    ret_f = lambda t, s: retention_scan(*t, s, lg_fwd)
    ret_b = lambda t, s: retention_scan(*t, s, lg_bwd)
    r_ctx_in = (c_rq, c_rk, c_rv)
    r_lat_in = (l_rq, l_rk, l_rv)
    r_ctx, r_lat = bidir(ret_f, ret_b, r_ctx_in, r_lat_in, r_ctx_in, r_lat_in,
                         jnp.zeros((B, RET_HEADS, RET_DK, RET_DV), jnp.float32))
    y_lat = even_output(g_lat, l_gr, r_lat, l_rg, norm_w, w_out)
    y_ctx = even_output(g_ctx, c_gr, r_ctx, c_rg, norm_w, w_out) if need_ctx else None
    return y_ctx, y_lat


def mla_project(h, w_in, q_norm_w, kv_norm_w, w_uq, w_ukv, axial):
    cq, ckv, k_rope = jnp.split(h @ w_in, [MLA_Q_RANK, MLA_Q_RANK + MLA_KV_RANK], axis=-1)
    q = to_heads(rms_norm(cq, q_norm_w) @ w_uq, MLA_HEADS)
    q_nope, q_rope = q[..., :MLA_NOPE], q[..., MLA_NOPE:]
    kv = to_heads(rms_norm(ckv, kv_norm_w) @ w_ukv, MLA_HEADS)
    k_nope, v = kv[..., :MLA_NOPE], kv[..., MLA_NOPE:]
    if axial is not None:
        q_rope = apply_axial(q_rope, axial)
        k_rope = apply_axial(k_rope, axial)
    return q_nope, q_rope, k_nope, k_rope, v


def mla_attend(qn, qr, kn, kr, v):
    s = jnp.einsum('bhqd,bhkd->bhqk', qn, kn) + jnp.einsum('bhqr,bkr->bhqk', qr, kr)
    p = jax.nn.softmax(s.astype(jnp.float32) * MLA_SCALE, axis=-1).astype(v.dtype)
    return jnp.einsum('bhqk,bhkd->bhqd', p, v)


def mla_blocked(qn, qr, kn, kr, v):
    B, H, T, _ = qn.shape
    nb = T // Q_BLOCK
    blk = lambda a: jnp.moveaxis(a.reshape(B, H, nb, Q_BLOCK, a.shape[-1]), 2, 0)
    o = lax.map(lambda qs: mla_attend(qs[0], qs[1], kn, kr, v), (blk(qn), blk(qr)))
    return jnp.moveaxis(o, 0, 2).reshape(B, H, T, MLA_DV)


def odd_mixer(h_ctx, h_lat, axial, w_in, q_norm_w, kv_norm_w, w_uq, w_ukv, w_out, need_ctx):
    qn_c, qr_c, kn_c, kr_c, v_c = mla_project(h_ctx, w_in, q_norm_w, kv_norm_w, w_uq, w_ukv, None)
    qn_l, qr_l, kn_l, kr_l, v_l = mla_project(h_lat, w_in, q_norm_w, kv_norm_w, w_uq, w_ukv, axial)
    kn = jnp.concatenate([kn_c, kn_l], axis=2)
    kr = jnp.concatenate([kr_c, kr_l], axis=1)
    v = jnp.concatenate([v_c, v_l], axis=2)
    y_lat = from_heads(mla_blocked(qn_l, qr_l, kn, kr, v)) @ w_out
    y_ctx = from_heads(mla_attend(qn_c, qr_c, kn_c, kr_c, v_c)) @ w_out if need_ctx else None
    return y_ctx, y_lat


def setup_inputs(seed: int = 0) -> dict:
    key = jax.random.key(seed)
    k = jax.random.split(key, 23)
    d = D_MODEL
    n_even = (DEPTH + 1) // 2
    n_odd = DEPTH // 2

    def nrm(i, shape, scale):
        return jax.random.normal(k[i], shape, jnp.float32) * scale

    return {
        'x': nrm(0, (BATCH, SEQ, d), 1.0),
        'c': nrm(1, (BATCH, d), 1.0),
        'ctx': nrm(2, (BATCH, CTX_LEN, d), 1.0),
        'c_ctx': nrm(3, (d,), 1.0),
        'w_ada': nrm(4, (DEPTH, d, N_MOD * d), d ** -0.5),
        'b_ada': nrm(5, (DEPTH, N_MOD * d), 0.02),
        'ln_g': 1.0 + nrm(6, (DEPTH, 3, d), 0.02),
        'ln_b': nrm(7, (DEPTH, 3, d), 0.02),
        'w_ffn_in': nrm(8, (DEPTH, 2, d, 2 * D_FF), d ** -0.5),
        'w_ffn_out': nrm(9, (DEPTH, 2, D_FF, d), D_FF ** -0.5 * DN_BETA),
        'ev_w_in': nrm(10, (n_even, d, EVEN_IN), d ** -0.5),
        'ev_gla_wg_f': nrm(11, (n_even, GLA_GATE_RANK, GLA_QK), GLA_GATE_RANK ** -0.5),
        'ev_gla_bg_f': nrm(12, (n_even, GLA_QK), 0.02),
        'ev_gla_wg_b': nrm(13, (n_even, GLA_GATE_RANK, GLA_QK), GLA_GATE_RANK ** -0.5),
        'ev_gla_bg_b': nrm(14, (n_even, GLA_QK), 0.02),
        'ev_gla_norm': 1.0 + nrm(15, (n_even, GLA_DV), 0.02),
        'ev_w_out': nrm(16, (n_even, EVEN_MIX, d), EVEN_MIX ** -0.5 * DN_BETA),
        'od_w_in': nrm(17, (n_odd, d, MLA_IN), d ** -0.5),
        'od_q_norm': 1.0 + nrm(18, (n_odd, MLA_Q_RANK), 0.02),
        'od_kv_norm': 1.0 + nrm(19, (n_odd, MLA_KV_RANK), 0.02),
        'od_w_uq': nrm(20, (n_odd, MLA_Q_RANK, MLA_HEADS * (MLA_NOPE + MLA_ROPE)), MLA_Q_RANK ** -0.5),
        'od_w_ukv': nrm(21, (n_odd, MLA_KV_RANK, MLA_HEADS * (MLA_NOPE + MLA_DV)), MLA_KV_RANK ** -0.5),
        'od_w_out': nrm(22, (n_odd, MLA_MIX, d), MLA_MIX ** -0.5 * DN_BETA),
    }


def reference(x, c, ctx, c_ctx, w_ada, b_ada, ln_g, ln_b, w_ffn_in, w_ffn_out,
              ev_w_in, ev_gla_wg_f, ev_gla_bg_f, ev_gla_wg_b, ev_gla_bg_b, ev_gla_norm, ev_w_out,
              od_w_in, od_q_norm, od_kv_norm, od_w_uq, od_w_ukv, od_w_out):
    n_lat = x.shape[1]
    axial = axial_tables(n_lat, MLA_ROPE)
    ret_rope = rope_tables(jnp.arange(n_lat), RET_DK)
    s_c = jax.nn.silu(c)
    s_cc = jax.nn.silu(c_ctx)
    x_lat, x_ctx = x, ctx
    for l in range(DEPTH):
        need_ctx = l < DEPTH - 1
        m_lat = jnp.split((s_c @ w_ada[l] + b_ada[l])[:, None, :], N_MOD, axis=-1)
        m_ctx = jnp.split((s_cc @ w_ada[l] + b_ada[l])[None, None, :], N_MOD, axis=-1)
        x_ctx = ffn_sublayer(x_ctx, *m_ctx[0:3], w_ffn_in[l, 0], w_ffn_out[l, 0], ln_g[l, 0], ln_b[l, 0])
        x_lat = ffn_sublayer(x_lat, *m_lat[0:3], w_ffn_in[l, 0], w_ffn_out[l, 0], ln_g[l, 0], ln_b[l, 0])
        h_ctx = modulate(x_ctx, m_ctx[3], m_ctx[4])
        h_lat = modulate(x_lat, m_lat[3], m_lat[4])
        if l % 2 == 0:
            e = l // 2
            y_ctx, y_lat = even_mixer(h_ctx, h_lat, ret_rope, ev_w_in[e], ev_gla_wg_f[e], ev_gla_bg_f[e],
                                      ev_gla_wg_b[e], ev_gla_bg_b[e], ev_gla_norm[e], ev_w_out[e], need_ctx)
        else:
            o = l // 2
            y_ctx, y_lat = odd_mixer(h_ctx, h_lat, axial, od_w_in[o], od_q_norm[o], od_kv_norm[o],
                                     od_w_uq[o], od_w_ukv[o], od_w_out[o], need_ctx)
        x_lat = layer_norm(DN_ALPHA * x_lat + m_lat[5] * y_lat, ln_g[l, 1], ln_b[l, 1])
        x_lat = ffn_sublayer(x_lat, *m_lat[6:9], w_ffn_in[l, 1], w_ffn_out[l, 1], ln_g[l, 2], ln_b[l, 2])
        if need_ctx:
            x_ctx = layer_norm(DN_ALPHA * x_ctx + m_ctx[5] * y_ctx, ln_g[l, 1], ln_b[l, 1])
            x_ctx = ffn_sublayer(x_ctx, *m_ctx[6:9], w_ffn_in[l, 1], w_ffn_out[l, 1], ln_g[l, 2], ln_b[l, 2])
    return x_lat
```

```python
import numpy as np
import concourse.bass as bass
import concourse.mybir as mybir
from concourse.bass_utils import run_bass_kernel_spmd

F32 = mybir.dt.float32
BF16 = mybir.dt.bfloat16
AF = mybir.ActivationFunctionType
ALU = mybir.AluOpType

NCORES = 8
D = 2048
KC = 16
DFF = 5504
NJ = 43
DEPTH = 4
CTX = 256
SEQ = 16384
LAT = SEQ // NCORES
NTOK = CTX + LAT
ALPHA = (2 * DEPTH) ** 0.25
LN_EPS = 1e-5
EPS_P = LN_EPS / (ALPHA * ALPHA)

ENGS = ("pe", "act", "dve", "pool", "sp")
EPOCH = 11000
DEPOCH = 700
NLANE = 32


class Sched:
    def __init__(self, nc):
        self.nc = nc
        self.q = {e: [] for e in ENGS}
        self.cnt = {e: 0 for e in ENGS}
        self.waited = {e: {} for e in ENGS}
        self.state = {}
        self.dcnt = {}
        self.semh = {}
        self.latest = {}
        self.lane_map = {}
        self.lane_cnt = [0] * NLANE

    def lane(self, name):
        if name not in self.lane_map:
            used = set(self.lane_map.values())
            cand = [i for i in range(NLANE) if i not in used]
            i = min(cand, key=lambda j: self.lane_cnt[j])
            self.lane_map[name] = i
        i = self.lane_map[name]
        self.lane_cnt[i] += 1
        return i

    def op(self, eng, fn, reads=(), writes=(), dma=None, inc=None):
        need = {}

        def add(tok, raw):
            if tok is None:
                return
            base, ep, val = tok
            if base == eng:
                if eng == "pe" or not raw:
                    return
            cur = need.get(base)
            if cur is None or (ep, val) > cur:
                need[base] = (ep, val)

        for r in reads:
            st = self.state.get(r)
            if st:
                add(st[0], True)
        for w in writes:
            st = self.state.get(w)
            if st:
                add(st[0], False)
                for t in st[1].values():
                    add(t, False)
        waits = []
        for base, (ep, val) in need.items():
            cur = self.waited[eng].get(base)
            if cur is not None and cur >= (ep, val):
                continue
            self.waited[eng][base] = (ep, val)
            waits.append((base, ep, val))
        if dma is not None:
            base = "d:cc" if dma == "cc" else "d:%d" % self.lane(dma)
            step = 16 if inc is None else inc
            n = self.dcnt.get(base, 0)
            self.dcnt[base] = n + 1
            ep, val = divmod(n, DEPOCH)
            tok = (base, ep, (val + 1) * step)
        else:
            step = 1
            n = self.cnt[eng]
            self.cnt[eng] = n + 1
            ep, val = divmod(n, EPOCH)
            tok = (eng, ep, val + 1)
        for r in reads:
            st = self.state.setdefault(r, [None, {}])
            st[1][tok[0]] = tok
        for w in writes:
            self.state[w] = [tok, {}]
        self.latest[tok[0]] = tok
        self.q[eng].append((waits, fn, tok, step))
        return tok

    def sem(self, base, ep):
        k = (base, ep)
        if k not in self.semh:
            nm = "s_" + base.replace(":", "_") + "_" + str(ep)
            self.semh[k] = self.nc.alloc_semaphore(nm)
        return self.semh[k]

    def emit(self):
        nc = self.nc
        for e in ENGS:
            for waits, fn, tok, step in self.q[e]:
                self.sem(tok[0], tok[1])
        names = {"pe": "tensor", "act": "scalar", "dve": "vector", "pool": "gpsimd", "sp": "sync"}
        with nc.Block() as block:
            for e in ENGS:
                def body(eobj, e=e):
                    for waits, fn, tok, step in self.q[e]:
                        for (b, ep, v) in waits:
                            eobj.wait_ge(self.sem(b, ep), v)
                        ins = fn(eobj)
                        ins.then_inc(self.sem(tok[0], tok[1]), step)
                    if e == "sp":
                        for b, (bb, ep, v) in self.latest.items():
                            eobj.wait_ge(self.sem(bb, ep), v)
                getattr(block, names[e])(body)


def dram_ap(t, offset, dims):
    return bass.AP(t, offset, [list(d) for d in dims])


class Prog:
    def __init__(self):
        self.nc = bass.Bass("TRN2", target_bir_lowering=False)
        self.S = Sched(self.nc)
        self.inputs = []
        self.wq = []
        self.uid = 0

    def name(self, p):
        self.uid += 1
        return f"{p}{self.uid}"

    def ext_in(self, name, shape, dt=F32):
        return self.nc.dram_tensor(name, list(shape), dt, kind="ExternalInput")

    def ext_out(self, name, shape, dt=F32):
        return self.nc.dram_tensor(name, list(shape), dt, kind="ExternalOutput")

    def dram(self, name, shape, dt):
        return self.nc.dram_tensor(name, list(shape), dt)

    def sb(self, name, shape, dt):
        return self.nc.alloc_sbuf_tensor(name, list(shape), dt)

    def ps(self, name, shape=(128, 512), dt=F32):
        return self.nc.alloc_psum_tensor(name, list(shape), dt)


WCH = 2048


def weight_pipeline(P, name, m, stage_f, stage_b):
    S = P.S
    src = P.ext_in(name, [128, m])
    loc = P.dram(name + "_loc", [128, m], BF16)
    full = P.dram(name + "_full", [NCORES * 128, m], BF16)
    nchunk = (m + WCH - 1) // WCH
    for c in range(nchunk):
        c0 = c * WCH
        w = min(WCH, m - c0)
        i = P.wctr % len(stage_f)
        P.wctr += 1
        sf, sbb = stage_f[i], stage_b[i]
        S.op("sp", lambda e, sf=sf, c0=c0, w=w: e.dma_start(out=sf[:, 0:w], in_=src[:, c0:c0 + w]),
             writes=[sf.name], dma="wl" + str(i))
        S.op("pool", lambda e, sf=sf, sbb=sbb, w=w: e.tensor_copy(out=sbb[:, 0:w], in_=sf[:, 0:w]),
             reads=[sf.name], writes=[sbb.name])
        S.op("sp", lambda e, sbb=sbb, c0=c0, w=w: e.dma_start(out=loc[:, c0:c0 + w], in_=sbb[:, 0:w]),
             reads=[sbb.name], writes=[loc.name], dma="ws" + str(i))
    S.op("pool", lambda e: e.collective_compute("AllGather", ALU.bypass, replica_groups=[list(range(NCORES))],
                                                ins=[loc.ap().opt()], outs=[full.ap().opt()]),
         reads=[loc.name], writes=[full.name], dma="cc", inc=1)
    return full


def barrier(S):
    lat = dict(S.latest)
    S.lane_map = {}
    for e in ENGS:
        waits = []
        for base, (b, ep, v) in lat.items():
            if base == e:
                continue
            cur = S.waited[e].get(base)
            if cur is not None and cur >= (ep, v):
                continue
            S.waited[e][base] = (ep, v)
            waits.append((b, ep, v))
        if waits:
            n = S.cnt[e]
            S.cnt[e] = n + 1
            ep, val = divmod(n, EPOCH)
            tok = (e, ep, val + 1)
            S.latest[e] = tok
            S.q[e].append((waits, (lambda eo: eo.nop()), tok, 1))


def shard_flat(arr):
    flat = np.ascontiguousarray(arr).reshape(-1)
    n = flat.size
    assert n % (NCORES * 128) == 0, n
    m = n // (NCORES * 128)
    return [flat[r * 128 * m:(r + 1) * 128 * m].reshape(128, m) for r in range(NCORES)], m


def col_layout(v):
    v = np.asarray(v)
    lead = v.shape[:-1]
    a = v.reshape(lead + (KC, 128))
    a = np.moveaxis(a, -1, 0)
    return np.ascontiguousarray(a)


def lay_w_in(w):
    a = w.reshape(KC, 128, 2, NJ, 128)
    return np.ascontiguousarray(a.transpose(3, 1, 0, 2, 4))


def lay_w_out(w):
    a = w.reshape(NJ, 128, KC, 128)
    return np.ascontiguousarray(a.transpose(2, 1, 0, 3))


def lay_w_ada(w_ada, r):
    a = w_ada.reshape(DEPTH, KC, 128, 9, 8, 2, 128)
    a = a[:, :, :, :, r]
    a = a.transpose(0, 3, 4, 2, 1, 5)
    return np.ascontiguousarray(a).reshape(DEPTH * 18, 128, KC * 128)


def lay_b_ada(b_ada, r):
    a = b_ada.reshape(DEPTH, 9, 8, 2, 128)[:, :, r]
    return np.ascontiguousarray(a.transpose(3, 0, 1, 2)).reshape(128, DEPTH * 18)


def build_ada(P, PS):
    S, nc = P.S, P.nc
    ccol = P.ext_in("ccol", [128, KC, 2])
    wada = P.ext_in("wada", [DEPTH * 18, 128, KC * 128])
    bada = P.ext_in("bada", [128, DEPTH * 18])
    modloc = P.dram("modloc", [128, 144], F32)
    modfull = P.dram("modfull", [NCORES * 128, 144], F32)
    MOD = P.sb("MOD", [128, NCORES, 144], F32)
    with nc.sbuf_tensor("ada_st", [128, KC, 2], F32) as ST, \
            nc.sbuf_tensor("ada_c", [128, KC, 2], F32) as CC, \
            nc.sbuf_tensor("ada_b", [128, DEPTH * 18], F32) as BA, \
            nc.sbuf_tensor("ada_m", [128, DEPTH, 9, 2, 2], F32) as ML, \
            nc.sbuf_tensor("ada_w0", [128, KC * 128], F32) as W0, \
            nc.sbuf_tensor("ada_w1", [128, KC * 128], F32) as W1, \
            nc.sbuf_tensor("ada_w2", [128, KC * 128], F32) as W2:
        WB = [W0, W1, W2]
        S.op("sp", lambda e: e.dma_start(out=CC[:], in_=ccol[:]), writes=[CC.name], dma="adac")
        S.op("sp", lambda e: e.dma_start(out=BA[:], in_=bada[:]), writes=[BA.name], dma="adab")
        S.op("act", lambda e: e.activation(out=ST[:], in_=CC[:], func=AF.Silu), reads=[CC.name], writes=[ST.name])
        ps = PS[0]
        for idx in range(DEPTH * 18):
            W = WB[idx % 3]
            S.op("sp", lambda e, W=W, idx=idx: e.dma_start(out=W[:], in_=wada[idx]), writes=[W.name],
                 dma="adaw%d" % (idx % 3))
            for kc in range(KC):
                S.op("pe", lambda e, W=W, idx=idx, kc=kc: e.matmul(
                    ps[:, 2 * idx:2 * idx + 2], lhsT=W[:, kc * 128:(kc + 1) * 128], rhs=ST[:, kc, :],
                    start=(kc == 0), stop=(kc == KC - 1)),
                    reads=[W.name, ST.name], writes=[ps.name])
        MLf = ML[:].rearrange("p l v h r -> p (l v h) r")
        psv = ps[:, 0:144].rearrange("p (i r) -> p i r", r=2)
        for row in range(2):
            S.op("dve", lambda e, row=row: e.tensor_tensor(out=MLf[:, :, row], in0=psv[:, :, row], in1=BA[:],
                                                           op=ALU.add),
                 reads=[ps.name, BA.name], writes=[ML.name])
        for v in (1, 4, 7):
            S.op("dve", lambda e, v=v: e.tensor_scalar_add(out=ML[:, :, v], in0=ML[:, :, v], scalar1=1.0),
                 reads=[ML.name], writes=[ML.name])
        for v, f in ((2, 0.5 / ALPHA), (8, 0.5 / ALPHA), (5, 1.0 / ALPHA)):
            S.op("dve", lambda e, v=v, f=f: e.tensor_scalar_mul(out=ML[:, :, v], in0=ML[:, :, v], scalar1=f),
                 reads=[ML.name], writes=[ML.name])
        S.op("sp", lambda e: e.dma_start(out=modloc[:, :], in_=ML[:].rearrange("p l v h r -> p (l v h r)")),
             reads=[ML.name], writes=[modloc.name], dma="adas")
        S.op("pool", lambda e: e.collective_compute("AllGather", ALU.bypass, replica_groups=[list(range(NCORES))],
                                                    ins=[modloc.ap().opt()], outs=[modfull.ap().opt()]),
             reads=[modloc.name], writes=[modfull.name], dma="cc", inc=1)
        S.op("sp", lambda e: e.dma_start(out=MOD[:], in_=dram_ap(modfull, 0, [[144, 128], [128 * 144, NCORES], [1, 144]])),
             reads=[modfull.name], writes=[MOD.name], dma="adam")
        barrier(S)

    def mod(l, v, kc, row):
        idx = ((l * 9 + v) * 2 + (kc % 2)) * 2 + row
        return MOD[:, kc // 2, idx:idx + 1]
    P.mod = mod
    P.MODname = MOD.name
    return mod


def resid_ln(P, PS, B, l, vg, ln_i, row, n, ymm):
    S = P.S
    mod = P.mod
    XG, SQ = B["XG"], B["SQ"]
    MEAN, VAR, RSTD, NMR, ONES, LNG, LNB = B["MEAN"], B["VAR"], B["RSTD"], B["NMR"], B["ONES"], B["LNG"], B["LNB"]
    Y, SUM, SSQ = [PS[4], PS[5]], PS[6], PS[7]

    def stats(c):
        sq = SQ[c % 2]
        S.op("pe", lambda e: e.matmul(SUM[:, 0:n], lhsT=ONES[:], rhs=XG[:, c, 0:n],
                                      start=(c == 0), stop=(c == KC - 1)),
             reads=[(XG.name, "t", c)], writes=[SUM.name])
        S.op("pe", lambda e: e.matmul(SSQ[:, 0:n], lhsT=ONES[:], rhs=sq[:, 0:n],
                                      start=(c == 0), stop=(c == KC - 1)),
             reads=[sq.name], writes=[SSQ.name])

    for c in range(KC):
        y = Y[c % 2]
        ymm(c, y)
        S.op("dve", lambda e, y=y, c=c: e.scalar_tensor_tensor(
            out=XG[:, c, 0:n], in0=y[:, 0:n], scalar=mod(l, vg, c, row), in1=XG[:, c, 0:n],
            op0=ALU.mult, op1=ALU.add),
            reads=[y.name, XG.name, P.MODname], writes=[(XG.name, "t", c)])
        sq = SQ[c % 2]
        S.op("act", lambda e, sq=sq, c=c: e.activation(out=sq[:, 0:n], in_=XG[:, c, 0:n], func=AF.Square),
             reads=[(XG.name, "t", c)], writes=[sq.name])
        if c >= 1:
            stats(c - 1)
    stats(KC - 1)
    inv = 1.0 / D
    S.op("dve", lambda e: e.tensor_scalar_mul(out=MEAN[:, 0:n], in0=SUM[:, 0:n], scalar1=inv),
         reads=[SUM.name], writes=[MEAN.name])
    S.op("dve", lambda e: e.tensor_tensor(out=VAR[:, 0:n], in0=MEAN[:, 0:n], in1=MEAN[:, 0:n], op=ALU.mult),
         reads=[MEAN.name], writes=[VAR.name])
    S.op("dve", lambda e: e.scalar_tensor_tensor(
        out=VAR[:, 0:n], in0=SSQ[:, 0:n], scalar=inv, in1=VAR[:, 0:n], op0=ALU.mult, op1=ALU.subtract),
        reads=[SSQ.name, VAR.name], writes=[VAR.name])
    S.op("dve", lambda e: e.tensor_scalar_add(out=VAR[:, 0:n], in0=VAR[:, 0:n], scalar1=EPS_P),
         reads=[VAR.name], writes=[VAR.name])
    S.op("act", lambda e: e.activation(out=RSTD[:, 0:n], in_=VAR[:, 0:n], func=AF.Sqrt),
         reads=[VAR.name], writes=[RSTD.name])
    S.op("dve", lambda e: e.reciprocal(out=RSTD[:, 0:n], in_=RSTD[:, 0:n]),
         reads=[RSTD.name], writes=[RSTD.name])
    S.op("dve", lambda e: e.scalar_tensor_tensor(
        out=NMR[:, 0:n], in0=MEAN[:, 0:n], scalar=-1.0, in1=RSTD[:, 0:n], op0=ALU.mult, op1=ALU.mult),
        reads=[MEAN.name, RSTD.name], writes=[NMR.name])
    for c in range(KC):
        S.op("dve", lambda e, c=c: e.tensor_tensor(out=XG[:, c, 0:n], in0=XG[:, c, 0:n], in1=RSTD[:, 0:n],
                                                   op=ALU.mult),
             reads=[(XG.name, "t", c), RSTD.name], writes=[(XG.name, "u", c)])
        S.op("pool", lambda e, c=c: e.tensor_tensor(out=XG[:, c, 0:n], in0=XG[:, c, 0:n], in1=NMR[:, 0:n],
                                                    op=ALU.add),
             reads=[(XG.name, "u", c), NMR.name], writes=[(XG.name, "w", c)])
        S.op("act", lambda e, c=c: e.activation(
            out=XG[:, c, 0:n], in_=XG[:, c, 0:n], func=AF.Identity,
            bias=LNB[:, l * 3 + ln_i, c:c + 1], scale=LNG[:, l * 3 + ln_i, c:c + 1]),
            reads=[(XG.name, "w", c)], writes=[(XG.name, "o", c)])


def lat_groups(include_ctx=True):
    g = [(0, CTX, 1)] if include_ctx else []
    g += [(CTX + 512 * i, 512, 0) for i in range(LAT // 512)]
    return g


def build_ffn(P, PS, B, l, f, wi_full, wo_full, XS, dst, dst_off, groups):
    S, nc = P.S, P.nc
    mod = P.mod
    vs, vc, vg = (0, 1, 2) if f == 0 else (6, 7, 8)
    ln_i = 0 if f == 0 else 2
    XG, HT, AT, WJ, WO, SG, SQ = B["XG"], B["HT"], B["AT"], B["WJ"], B["WO"], B["SG"], B["SQ"]
    ZG, ZU = [PS[0], PS[1]], [PS[2], PS[3]]
    ntok_src = XS.shape[2]
    ntok_dst = dst.shape[2]
    wj_i = 0
    wo_i = 0
    for (t0, n, row) in groups:
        S.op("sp", lambda e, t0=t0, n=n: e.dma_start(
            out=XG[:, :, 0:n], in_=dram_ap(XS, t0, [[ntok_src, 128], [128 * ntok_src, KC], [1, n]])),
            reads=[XS.name], writes=[XG.name], dma="xg")
        for kc in range(KC):
            S.op("act", lambda e, kc=kc, n=n, row=row: e.activation(
                out=HT[:, kc, 0:n], in_=XG[:, kc, 0:n], func=AF.Identity,
                bias=mod(l, vs, kc, row), scale=mod(l, vc, kc, row)),
                reads=[XG.name, P.MODname], writes=[(HT.name, kc)])
        for j in range(NJ):
            W = WJ[wj_i % len(WJ)]
            S.op("sp", lambda e, W=W, j=j: e.dma_start(
                out=W[:], in_=dram_ap(wi_full, j * 128 * KC * 256, [[KC * 256, 128], [1, KC * 256]])),
                reads=[wi_full.name], writes=[W.name], dma="wj%d" % (wj_i % len(WJ)))
            wj_i += 1
            zg, zu = ZG[j % 2], ZU[j % 2]
            for (z, off) in ((zg, 0), (zu, 128)):
                for kc in range(KC):
                    S.op("pe", lambda e, z=z, W=W, kc=kc, off=off, n=n: e.matmul(
                        z[:, 0:n], lhsT=W[:, kc * 256 + off:kc * 256 + off + 128], rhs=HT[:, kc, 0:n],
                        start=(kc == 0), stop=(kc == KC - 1)),
                        reads=[W.name, (HT.name, kc)], writes=[z.name])
            sg = SG[j % 2]
            S.op("act", lambda e, sg=sg, zg=zg, n=n: e.activation(out=sg[:, 0:n], in_=zg[:, 0:n], func=AF.Silu),
                 reads=[zg.name], writes=[sg.name])
            S.op("dve", lambda e, sg=sg, zu=zu, j=j, n=n: e.tensor_tensor(
                out=AT[:, j, 0:n], in0=sg[:, 0:n], in1=zu[:, 0:n], op=ALU.mult),
                reads=[sg.name, zu.name], writes=[(AT.name, j)])

        def ymm(c, y, n=n):
            nonlocal wo_i
            W = WO[wo_i % len(WO)]
            S.op("sp", lambda e, W=W, c=c: e.dma_start(
                out=W[:], in_=dram_ap(wo_full, c * 128 * NJ * 128, [[NJ * 128, 128], [1, NJ * 128]])),
                reads=[wo_full.name], writes=[W.name], dma="wo%d" % (wo_i % len(WO)))
            wo_i += 1
            for j in range(NJ):
                S.op("pe", lambda e, y=y, W=W, j=j, n=n: e.matmul(
                    y[:, 0:n], lhsT=W[:, j * 128:(j + 1) * 128], rhs=AT[:, j, 0:n],
                    start=(j == 0), stop=(j == NJ - 1)),
                    reads=[W.name, (AT.name, j)], writes=[y.name])
        resid_ln(P, PS, B, l, vg, ln_i, row, n, ymm)
        S.op("sp", lambda e, t0=t0, n=n: e.dma_start(
            out=dram_ap(dst, t0 - dst_off, [[ntok_dst, 128], [128 * ntok_dst, KC], [1, n]]), in_=XG[:, :, 0:n]),
            reads=[XG.name] + [(XG.name, "o", c) for c in range(KC)], writes=[dst.name], dma="xg")
        S.state[XG.name] = [S.latest["d:%d" % S.lane_map["xg"]], {}]


MLA_H = 16
MLA_SCALE = 192 ** -0.5
RMS_EPS = 1e-6


def _swap_perm(dim_half):
    return np.concatenate([np.arange(dim_half, 2 * dim_half), np.arange(0, dim_half)])


def lay_mla_weights(w_in, w_uq, w_ukv, w_out):
    perm64 = np.concatenate([_swap_perm(16), 32 + _swap_perm(16)])
    a = w_in[:, :1024].reshape(KC, 128, 8, 128)
    wina = np.ascontiguousarray(a.transpose(2, 1, 0, 3))
    kr = w_in[:, 1024:1088]
    krs = kr[:, perm64]
    b = np.stack([kr, krs], 0).reshape(2, KC, 128, 64)
    winr = np.ascontiguousarray(b.transpose(0, 2, 1, 3))
    q = w_uq.reshape(4, 128, MLA_H, 192)
    wuqn = np.ascontiguousarray(q[:, :, :, :128].transpose(2, 1, 0, 3))
    qr = q[:, :, :, 128:]
    qrs = qr[:, :, :, perm64]
    wuqr = np.ascontiguousarray(np.stack([qr, qrs], 0).transpose(3, 0, 2, 1, 4))
    kv = w_ukv.reshape(4, 128, MLA_H, 256)
    wukn = np.ascontiguousarray(kv[:, :, :, :128].transpose(2, 1, 0, 3))
    wukv = np.ascontiguousarray(kv[:, :, :, 128:].transpose(1, 0, 2, 3))
    o = w_out.reshape(MLA_H, 128, KC, 128)
    wout = np.ascontiguousarray(o.transpose(2, 1, 0, 3))
    return {"wina": wina, "winr": winr, "wuqn": wuqn, "wuqr": wuqr, "wukn": wukn, "wukv": wukv, "wout": wout}


def axial_tables_np(r):
    t = np.arange(r * LAT, (r + 1) * LAT)
    row = (t // 64).astype(np.float32)
    col = (t % 64).astype(np.float32)
    inv = (np.float32(10000.0) ** (-(np.arange(0, 32, 2, dtype=np.float32)) / np.float32(32))).astype(np.float32)

    def tab(pos):
        ang = pos[:, None] * inv[None, :]
        ang = np.concatenate([ang, ang], -1)
        return np.cos(ang).astype(np.float32), np.sin(ang).astype(np.float32)
    cr, sr = tab(row)
    cc, sc = tab(col)
    cos = np.concatenate([cr, cc], -1)
    sin = np.concatenate([sr, sc], -1)
    sign = np.concatenate([-np.ones(16), np.ones(16), -np.ones(16), np.ones(16)]).astype(np.float32)
    sin = sin * sign[None, :]
    cosT = np.concatenate([np.ones((64, CTX), np.float32), cos.T], 1)
    sinT = np.concatenate([np.zeros((64, CTX), np.float32), sin.T], 1)
    return np.ascontiguousarray(cosT), np.ascontiguousarray(sinT)


def build_mla(P, PS, B, l, W, XS, need_ctx, tabs):
    S, nc = P.S, P.nc
    mod = P.mod
    o_idx = l // 2
    XG, HT, SQ, ONES = B["XG"], B["HT"], B["SQ"], B["ONES"]
    sfx = "_%d" % l
    KN_loc = P.dram("kn_loc" + sfx, [MLA_H * 128, LAT], BF16)
    KN_full = P.dram("kn_full" + sfx, [NCORES * MLA_H * 128, LAT], BF16)
    KN_ctx = P.dram("kn_ctx" + sfx, [MLA_H * 128, CTX], BF16)
    V_loc = P.dram("v_loc" + sfx, [LAT, MLA_H * 128], BF16)
    V_full = P.dram("v_full" + sfx, [NCORES * LAT, MLA_H * 128], BF16)
    V_ctx = P.dram("v_ctx" + sfx, [CTX, MLA_H * 128], BF16)
    KR_loc = P.dram("kr_loc" + sfx, [64, LAT], BF16)
    KR_full = P.dram("kr_full" + sfx, [NCORES * 64, LAT], BF16)
    KR_ctx = P.dram("kr_ctx" + sfx, [64, CTX], BF16)
    QN_loc = P.dram("qn_loc" + sfx, [MLA_H * 128, NTOK], BF16)
    QR_loc = P.dram("qr_loc" + sfx, [MLA_H * 64, NTOK], BF16)
    import contextlib
    with contextlib.ExitStack() as es:
        def sbt(name, shape, dt):
            return es.enter_context(nc.sbuf_tensor(name + sfx, list(shape), dt))
        COS = sbt("COS", [64, NTOK], F32)
        SIN = sbt("SIN", [64, NTOK], F32)
        NW = sbt("NW", [128, 2, 4], F32)
        C32 = sbt("C32", [128, 8, 512], F32)
        CN = sbt("CN", [128, 8, 512], BF16)
        R2 = sbt("R2", [128, 2, 512], F32)
        KR32 = sbt("KR32", [64, 2, 512], F32)
        T64 = sbt("T64", [64, 512], F32)
        WA = [sbt("WA%d" % i, [128, KC * 128], BF16) for i in range(3)]
        WR = [sbt("WR%d" % i, [128, KC * 64], BF16) for i in range(2)]
        WS = [sbt("WS%d" % i, [128, 4 * 128], BF16) for i in range(3)]
        WRS = [sbt("WRS%d" % i, [128, 2, 4 * 64], BF16) for i in range(2)]
        WV = sbt("WV", [128, 4, MLA_H * 128], BF16)
        OB = [sbt("OB%d" % i, [128, 512], BF16) for i in range(3)]
        OB64 = [sbt("OB64%d" % i, [64, 512], BF16) for i in range(2)]
        VB = [sbt("VBo%d" % i, [128, MLA_H * 128], BF16) for i in range(2)]
        S.op("sp", lambda e: e.dma_start(out=COS[:], in_=tabs[0][:, :]), writes=[COS.name], dma="mc0")
        S.op("sp", lambda e: e.dma_start(out=SIN[:], in_=tabs[1][:, :]), writes=[SIN.name], dma="mc1")
        S.op("sp", lambda e: e.dma_start(out=NW[:], in_=tabs[2][o_idx]), writes=[NW.name], dma="mc2")
        S.op("sp", lambda e: e.dma_start(out=WV[:], in_=dram_ap(W["wukv"], 0, [[4 * 2048, 128], [1, 4 * 2048]])),
             reads=[W["wukv"].name], writes=[WV.name], dma="mc3")
        cnt = {"wa": 0, "wr": 0, "ws": 0, "wrs": 0, "ob": 0, "ob64": 0, "vb": 0, "ps": 0}

        def bank():
            b = PS[cnt["ps"] % 6]
            cnt["ps"] += 1
            return b
        groups = lat_groups(True)
        for (t0, n, row) in groups:
            S.op("sp", lambda e, t0=t0, n=n: e.dma_start(
                out=XG[:, :, 0:n], in_=dram_ap(XS, t0, [[NTOK, 128], [128 * NTOK, KC], [1, n]])),
                reads=[XS.name], writes=[XG.name], dma="xg")
            for kc in range(KC):
                S.op("act", lambda e, kc=kc, n=n, row=row: e.activation(
                    out=HT[:, kc, 0:n], in_=XG[:, kc, 0:n], func=AF.Identity,
                    bias=mod(l, 3, kc, row), scale=mod(l, 4, kc, row)),
                    reads=[XG.name, P.MODname], writes=[(HT.name, kc)])
            SUMS = [PS[6], PS[7]]
            for ch in range(8):
                Wt = WA[cnt["wa"] % 3]
                S.op("sp", lambda e, Wt=Wt, ch=ch: e.dma_start(
                    out=Wt[:], in_=dram_ap(W["wina"], ch * 128 * KC * 128, [[KC * 128, 128], [1, KC * 128]])),
                    reads=[W["wina"].name], writes=[Wt.name], dma="mwa%d" % (cnt["wa"] % 3))
                cnt["wa"] += 1
                pb = bank()
                for kc in range(KC):
                    S.op("pe", lambda e, pb=pb, Wt=Wt, kc=kc, n=n: e.matmul(
                        pb[:, 0:n], lhsT=Wt[:, kc * 128:(kc + 1) * 128], rhs=HT[:, kc, 0:n],
                        start=(kc == 0), stop=(kc == KC - 1)),
                        reads=[Wt.name, (HT.name, kc)], writes=[pb.name])
                S.op("act", lambda e, pb=pb, ch=ch, n=n: e.activation(out=C32[:, ch, 0:n], in_=pb[:, 0:n], func=AF.Copy),
                     reads=[pb.name], writes=[(C32.name, ch)])
                sq = SQ[ch % 2]
                S.op("act", lambda e, pb=pb, sq=sq, n=n: e.activation(out=sq[:, 0:n], in_=pb[:, 0:n], func=AF.Square),
                     reads=[pb.name], writes=[sq.name])
                sm = SUMS[ch // 4]
                S.op("pe", lambda e, sm=sm, sq=sq, ch=ch, n=n: e.matmul(
                    sm[:, 0:n], lhsT=ONES[:], rhs=sq[:, 0:n], start=(ch % 4 == 0), stop=(ch % 4 == 3)),
                    reads=[sq.name], writes=[sm.name])
            for s_ in range(2):
                Wt = WR[cnt["wr"] % 2]
                S.op("sp", lambda e, Wt=Wt, s_=s_: e.dma_start(
                    out=Wt[:], in_=dram_ap(W["winr"], s_ * 128 * KC * 64, [[KC * 64, 128], [1, KC * 64]])),
                    reads=[W["winr"].name], writes=[Wt.name], dma="mwr%d" % (cnt["wr"] % 2))
                cnt["wr"] += 1
                pb = bank()
                for kc in range(KC):
                    S.op("pe", lambda e, pb=pb, Wt=Wt, kc=kc, n=n: e.matmul(
                        pb[0:64, 0:n], lhsT=Wt[:, kc * 64:(kc + 1) * 64], rhs=HT[:, kc, 0:n],
                        start=(kc == 0), stop=(kc == KC - 1)),
                        reads=[Wt.name, (HT.name, kc)], writes=[pb.name])
                S.op("act", lambda e, pb=pb, s_=s_, n=n: e.activation(out=KR32[:, s_, 0:n], in_=pb[0:64, 0:n], func=AF.Copy),
                     reads=[pb.name], writes=[(KR32.name, s_)])

            def rope64(src32, dst_bf, t0=t0, n=n):
                S.op("dve", lambda e: e.tensor_tensor(out=src32[:, 0, 0:n], in0=src32[:, 0, 0:n], in1=COS[:, t0:t0 + n], op=ALU.mult),
                     reads=[(src32.name, 0), COS.name], writes=[(src32.name, 0)])
                S.op("dve", lambda e: e.tensor_tensor(out=T64[:, 0:n], in0=src32[:, 1, 0:n], in1=SIN[:, t0:t0 + n], op=ALU.mult),
                     reads=[(src32.name, 1), SIN.name], writes=[T64.name])
                S.op("dve", lambda e: e.tensor_tensor(out=dst_bf[:, 0:n], in0=src32[:, 0, 0:n], in1=T64[:, 0:n], op=ALU.add),
                     reads=[(src32.name, 0), T64.name], writes=[dst_bf.name])
            ob = OB64[cnt["ob64"] % 2]
            cnt["ob64"] += 1
            rope64(KR32, ob)
            if row == 1:
                S.op("sp", lambda e, ob=ob, n=n: e.dma_start(out=KR_ctx[:, 0:n], in_=ob[:, 0:n]),
                     reads=[ob.name], writes=[KR_ctx.name], dma="mob64%d" % ((cnt["ob64"] - 1) % 2))
            else:
                S.op("sp", lambda e, ob=ob, n=n, t0=t0: e.dma_start(out=KR_loc[:, t0 - CTX:t0 - CTX + n], in_=ob[:, 0:n]),
                     reads=[ob.name], writes=[KR_loc.name], dma="mob64%d" % ((cnt["ob64"] - 1) % 2))
            for w_ in range(2):
                sm = SUMS[w_]
                S.op("dve", lambda e, sm=sm, w_=w_, n=n: e.tensor_scalar(
                    out=R2[:, w_, 0:n], in0=sm[:, 0:n], scalar1=1.0 / 512, scalar2=RMS_EPS, op0=ALU.mult, op1=ALU.add),
                    reads=[sm.name], writes=[(R2.name, w_)])
                S.op("act", lambda e, w_=w_, n=n: e.activation(out=R2[:, w_, 0:n], in_=R2[:, w_, 0:n], func=AF.Sqrt),
                     reads=[(R2.name, w_)], writes=[(R2.name, w_)])
                S.op("dve", lambda e, w_=w_, n=n: e.reciprocal(out=R2[:, w_, 0:n], in_=R2[:, w_, 0:n]),
                     reads=[(R2.name, w_)], writes=[(R2.name, w_)])
            for ch in range(8):
                S.op("dve", lambda e, ch=ch, n=n: e.scalar_tensor_tensor(
                    out=CN[:, ch, 0:n], in0=C32[:, ch, 0:n], scalar=NW[:, ch // 4, ch % 4:ch % 4 + 1],
                    in1=R2[:, ch // 4, 0:n], op0=ALU.mult, op1=ALU.mult),
                    reads=[(C32.name, ch), (R2.name, ch // 4), NW.name], writes=[(CN.name, ch)])

            def small_proj(wname, widx, wcols, cn_base, M):
                Wt = WS[cnt["ws"] % 3]
                S.op("sp", lambda e: e.dma_start(
                    out=Wt[:, 0:4 * wcols], in_=dram_ap(W[wname], widx * 128 * 4 * wcols, [[4 * wcols, 128], [1, 4 * wcols]])),
                    reads=[W[wname].name], writes=[Wt.name], dma="mws%d" % (cnt["ws"] % 3))
                cnt["ws"] += 1
                pb = bank()
                for c in range(4):
                    S.op("pe", lambda e, c=c: e.matmul(
                        pb[0:M, 0:n], lhsT=Wt[:, c * wcols:(c + 1) * wcols], rhs=CN[:, cn_base + c, 0:n],
                        start=(c == 0), stop=(c == 3)),
                        reads=[Wt.name, (CN.name, cn_base + c)], writes=[pb.name])
                return pb
            for h in range(MLA_H):
                pb = small_proj("wukn", h, 128, 4, 128)
                ob = OB[cnt["ob"] % 3]
                lane = "mob%d" % (cnt["ob"] % 3)
                cnt["ob"] += 1
                S.op("act", lambda e, pb=pb, ob=ob, n=n: e.activation(out=ob[:, 0:n], in_=pb[:, 0:n], func=AF.Copy),
                     reads=[pb.name], writes=[ob.name])
                if row == 1:
                    S.op("sp", lambda e, ob=ob, h=h, n=n: e.dma_start(out=KN_ctx[h * 128:(h + 1) * 128, 0:n], in_=ob[:, 0:n]),
                         reads=[ob.name], writes=[KN_ctx.name], dma=lane)
                else:
                    S.op("sp", lambda e, ob=ob, h=h, n=n, t0=t0: e.dma_start(
                        out=KN_loc[h * 128:(h + 1) * 128, t0 - CTX:t0 - CTX + n], in_=ob[:, 0:n]),
                        reads=[ob.name], writes=[KN_loc.name], dma=lane)
                pb = small_proj("wuqn", h, 128, 0, 128)
                ob = OB[cnt["ob"] % 3]
                lane = "mob%d" % (cnt["ob"] % 3)
                cnt["ob"] += 1
                S.op("act", lambda e, pb=pb, ob=ob, n=n: e.activation(out=ob[:, 0:n], in_=pb[:, 0:n], func=AF.Copy),
                     reads=[pb.name], writes=[ob.name])
                S.op("sp", lambda e, ob=ob, h=h, n=n, t0=t0: e.dma_start(
                    out=QN_loc[h * 128:(h + 1) * 128, t0:t0 + n], in_=ob[:, 0:n]),
                    reads=[ob.name], writes=[QN_loc.name], dma=lane)
                for s_ in range(2):
                    pb = small_proj("wuqr", h * 2 + s_, 64, 0, 64)
                    S.op("act", lambda e, pb=pb, s_=s_, n=n: e.activation(out=KR32[:, s_, 0:n], in_=pb[0:64, 0:n], func=AF.Copy),
                         reads=[pb.name], writes=[(KR32.name, s_)])
                ob = OB64[cnt["ob64"] % 2]
                lane = "mob64%d" % (cnt["ob64"] % 2)
                cnt["ob64"] += 1
                rope64(KR32, ob)
                S.op("sp", lambda e, ob=ob, h=h, n=n, t0=t0: e.dma_start(
                    out=QR_loc[h * 64:(h + 1) * 64, t0:t0 + n], in_=ob[:, 0:n]),
                    reads=[ob.name], writes=[QR_loc.name], dma=lane)
            for tt in range(n // 128):
                vb = VB[cnt["vb"] % 2]
                lane = "mvb%d" % (cnt["vb"] % 2)
                cnt["vb"] += 1
                for q4 in range(4):
                    pb = bank()
                    for c in range(4):
                        S.op("pe", lambda e, pb=pb, c=c, tt=tt, q4=q4: e.matmul(
                            pb[:, 0:512], lhsT=CN[:, 4 + c, tt * 128:(tt + 1) * 128], rhs=WV[:, c, q4 * 512:(q4 + 1) * 512],
                            start=(c == 0), stop=(c == 3)),
                            reads=[(CN.name, 4 + c), WV.name], writes=[pb.name])
                    S.op("act", lambda e, pb=pb, vb=vb, q4=q4: e.activation(out=vb[:, q4 * 512:(q4 + 1) * 512], in_=pb[:, 0:512], func=AF.Copy),
                         reads=[pb.name], writes=[vb.name])
                tg = t0 + tt * 128
                if row == 1:
                    S.op("sp", lambda e, vb=vb, tg=tg: e.dma_start(out=V_ctx[tg:tg + 128, :], in_=vb[:]),
                         reads=[vb.name], writes=[V_ctx.name], dma=lane)
                else:
                    S.op("sp", lambda e, vb=vb, tg=tg: e.dma_start(out=V_loc[tg - CTX:tg - CTX + 128, :], in_=vb[:]),
                         reads=[vb.name], writes=[V_loc.name], dma=lane)
        for (loc, full) in ((KN_loc, KN_full), (V_loc, V_full), (KR_loc, KR_full)):
            S.op("pool", lambda e, loc=loc, full=full: e.collective_compute(
                "AllGather", ALU.bypass, replica_groups=[list(range(NCORES))],
                ins=[loc.ap().opt()], outs=[full.ap().opt()]),
                reads=[loc.name], writes=[full.name], dma="cc", inc=1)
        barrier(S)
    return dict(KN_full=KN_full, KN_ctx=KN_ctx, V_full=V_full, V_ctx=V_ctx, KR_full=KR_full, KR_ctx=KR_ctx,
                QN_loc=QN_loc, QR_loc=QR_loc)


def build_mla_attn(P, PS, B, l, W, XS, need_ctx, T):
    S, nc = P.S, P.nc
    XG, ONESB = B["XG"], B["ONESB"]
    sfx = "_%d" % l
    OT_d = P.dram("ot_d" + sfx, [MLA_H * 128, NTOK], BF16)
    import contextlib
    with contextlib.ExitStack() as es:
        def sbt(name, shape, dt):
            return es.enter_context(nc.sbuf_tensor(name + sfx, list(shape), dt))
        QN = [sbt("QN%d" % i, [128, NTOK], BF16) for i in range(2)]
        QR = [sbt("QR%d" % i, [64, NTOK], BF16) for i in range(2)]
        KNB = [sbt("KNB%d" % i, [128, LAT], BF16) for i in range(2)]
        KRB = [sbt("KRB%d" % i, [64, LAT], BF16) for i in range(2)]
        VBK = [sbt("VBK%d" % i, [128, 16, 128], BF16) for i in range(2)]
        PT = [sbt("PT%d" % i, [128, 512], BF16) for i in range(4)]
        RC = [sbt("RC%d" % i, [128, 512], F32) for i in range(2)]
        OO = [sbt("OO%d" % i, [128, 512], BF16) for i in range(2)]
        O, DEN, OC, DC, ST = [PS[0], PS[1]], [PS[2], PS[3]], PS[4], PS[5], [PS[6], PS[7]]
        gblk = 0
        gitem = 0
        gfin = 0
        for h in range(MLA_H):
            qn, qr = QN[h % 2], QR[h % 2]
            S.op("sp", lambda e, qn=qn, h=h: e.dma_start(out=qn[:], in_=T["QN_loc"][h * 128:(h + 1) * 128, :]),
                 reads=[T["QN_loc"].name], writes=[qn.name], dma="aqn%d" % (h % 2))
            S.op("sp", lambda e, qr=qr, h=h: e.dma_start(out=qr[:], in_=T["QR_loc"][h * 64:(h + 1) * 64, :]),
                 reads=[T["QR_loc"].name], writes=[qr.name], dma="aqr%d" % (h % 2))
            for pss in range(2):
                qgroups = [(CTX + 512 * (2 * pss + i), 512, O[i], DEN[i]) for i in range(2)]
                ctxq = need_ctx and pss == 0
                items = []
                for blk in range(-1, NCORES):
                    nk = CTX if blk < 0 else LAT
                    for c in range(nk // 128):
                        for (q0, n, ob, db) in qgroups:
                            items.append((blk, c, q0, n, ob, db, blk < 0 and c == 0, blk == NCORES - 1 and c == 15))
                        if ctxq and blk < 0:
                            items.append((blk, c, 0, CTX, OC, DC, c == 0, c == 1))
                cur = {"blk": None}

                def load_block(blk, h=h):
                    nonlocal gblk
                    i = gblk % 2
                    gblk += 1
                    kn, kr, vb = KNB[i], KRB[i], VBK[i]
                    if blk < 0:
                        S.op("sp", lambda e: e.dma_start(out=kn[:, 0:CTX], in_=T["KN_ctx"][h * 128:(h + 1) * 128, :]),
                             reads=[T["KN_ctx"].name], writes=[kn.name], dma="akn%d" % i)
                        S.op("sp", lambda e: e.dma_start(out=kr[:, 0:CTX], in_=T["KR_ctx"][:, :]),
                             reads=[T["KR_ctx"].name], writes=[kr.name], dma="akr%d" % i)
                        S.op("sp", lambda e: e.dma_start(
                            out=vb[:, 0:2, :], in_=dram_ap(T["V_ctx"], h * 128, [[2048, 128], [128 * 2048, 2], [1, 128]])),
                            reads=[T["V_ctx"].name], writes=[vb.name], dma="avb%d" % i)
                    else:
                        S.op("sp", lambda e: e.dma_start(
                            out=kn[:], in_=dram_ap(T["KN_full"], (blk * 2048 + h * 128) * LAT, [[LAT, 128], [1, LAT]])),
                            reads=[T["KN_full"].name], writes=[kn.name], dma="akn%d" % i)
                        S.op("sp", lambda e: e.dma_start(
                            out=kr[:], in_=dram_ap(T["KR_full"], blk * 64 * LAT, [[LAT, 64], [1, LAT]])),
                            reads=[T["KR_full"].name], writes=[kr.name], dma="akr%d" % i)
                        S.op("sp", lambda e: e.dma_start(
                            out=vb[:], in_=dram_ap(T["V_full"], blk * LAT * 2048 + h * 128, [[2048, 128], [128 * 2048, 16], [1, 128]])),
                            reads=[T["V_full"].name], writes=[vb.name], dma="avb%d" % i)
                    return kn, kr, vb

                staged = []
                for k in range(len(items) + 1):
                    if k < len(items):
                        blk, c, q0, n, ob, db, first, last = items[k]
                        if cur["blk"] != blk:
                            cur["blk"] = blk
                            cur["bufs"] = load_block(blk)
                        kn, kr, vb = cur["bufs"]
                        st = ST[gitem % 2]
                        pt = PT[gitem % 4]
                        gitem += 1
                        S.op("pe", lambda e, st=st, kn=kn, c=c, q0=q0, n=n, qn=qn: e.matmul(
                            st[:, 0:n], lhsT=kn[:, c * 128:(c + 1) * 128], rhs=qn[:, q0:q0 + n], start=True, stop=False),
                            reads=[kn.name, qn.name], writes=[st.name])
                        S.op("pe", lambda e, st=st, kr=kr, c=c, q0=q0, n=n, qr=qr: e.matmul(
                            st[:, 0:n], lhsT=kr[:, c * 128:(c + 1) * 128], rhs=qr[:, q0:q0 + n], start=False, stop=True),
                            reads=[kr.name, qr.name], writes=[st.name])
                        S.op("act", lambda e, st=st, pt=pt, n=n: e.activation(out=pt[:, 0:n], in_=st[:, 0:n], func=AF.Exp, scale=MLA_SCALE),
                             reads=[st.name], writes=[pt.name])
                        staged.append((pt, vb, c, n, ob, db, first, last))
                    if k >= 1:
                        pt, vb, c, n, ob, db, first, last = staged[k - 1]
                        S.op("pe", lambda e, pt=pt, vb=vb, c=c, n=n, ob=ob, first=first, last=last: e.matmul(
                            ob[:, 0:n], lhsT=vb[:, c, :], rhs=pt[:, 0:n], start=first, stop=last),
                            reads=[vb.name, pt.name], writes=[ob.name])
                        S.op("pe", lambda e, pt=pt, n=n, db=db, first=first, last=last: e.matmul(
                            db[:, 0:n], lhsT=ONESB[:], rhs=pt[:, 0:n], start=first, stop=last),
                            reads=[pt.name], writes=[db.name])
                fin = [(q0, n, ob, db) for (q0, n, ob, db) in qgroups]
                if ctxq:
                    fin.append((0, CTX, OC, DC))
                for (q0, n, ob, db) in fin:
                    rc, oo = RC[gfin % 2], OO[gfin % 2]
                    lane = "aoo%d" % (gfin % 2)
                    gfin += 1
                    S.op("dve", lambda e, rc=rc, db=db, n=n: e.reciprocal(out=rc[:, 0:n], in_=db[:, 0:n]),
                         reads=[db.name], writes=[rc.name])
                    S.op("dve", lambda e, rc=rc, ob=ob, oo=oo, n=n: e.tensor_tensor(out=oo[:, 0:n], in0=ob[:, 0:n], in1=rc[:, 0:n], op=ALU.mult),
                         reads=[ob.name, rc.name], writes=[oo.name])
                    S.op("sp", lambda e, oo=oo, h=h, q0=q0, n=n: e.dma_start(out=OT_d[h * 128:(h + 1) * 128, q0:q0 + n], in_=oo[:, 0:n]),
                         reads=[oo.name], writes=[OT_d.name], dma=lane)
        barrier(S)
    def loader(e, OTG, t0, n):
        return e.dma_start(out=OTG[:, :, 0:n], in_=dram_ap(OT_d, t0, [[NTOK, 128], [128 * NTOK, MLA_H], [1, n]]))
    build_outproj(P, PS, B, l, W["wout"], XS, need_ctx, loader, OT_d.name)


def build_outproj(P, PS, B, l, wout, XS, need_ctx, loader, srcname):
    S, nc = P.S, P.nc
    XG = B["XG"]
    sfx = "_%d" % l
    import contextlib
    with contextlib.ExitStack() as es:
        def sbt(name, shape, dt):
            return es.enter_context(nc.sbuf_tensor(name + sfx, list(shape), dt))
        OTG = sbt("OTG", [128, 16, 512], BF16)
        WOT = [sbt("WOT%d" % i, [128, 16 * 128], BF16) for i in range(2)]
        wi = 0
        for (t0, n, row) in lat_groups(need_ctx):
            S.op("sp", lambda e, t0=t0, n=n: e.dma_start(
                out=XG[:, :, 0:n], in_=dram_ap(XS, t0, [[NTOK, 128], [128 * NTOK, KC], [1, n]])),
                reads=[XS.name], writes=[XG.name], dma="xg")
            S.op("sp", lambda e, t0=t0, n=n: loader(e, OTG, t0, n),
                 reads=[srcname], writes=[OTG.name], dma="otg")

            def ymm(c, y, n=n):
                nonlocal wi
                Wt = WOT[wi % 2]
                S.op("sp", lambda e: e.dma_start(
                    out=Wt[:], in_=dram_ap(wout, c * 128 * 16 * 128, [[16 * 128, 128], [1, 16 * 128]])),
                    reads=[wout.name], writes=[Wt.name], dma="wot%d" % (wi % 2))
                wi += 1
                for h in range(16):
                    S.op("pe", lambda e, h=h: e.matmul(
                        y[:, 0:n], lhsT=Wt[:, h * 128:(h + 1) * 128], rhs=OTG[:, h, 0:n],
                        start=(h == 0), stop=(h == 15)),
                        reads=[Wt.name, OTG.name], writes=[y.name])
            resid_ln(P, PS, B, l, 5, 1, row, n, ymm)
            S.op("sp", lambda e, t0=t0, n=n: e.dma_start(
                out=dram_ap(XS, t0, [[NTOK, 128], [128 * NTOK, KC], [1, n]]), in_=XG[:, :, 0:n]),
                reads=[XG.name] + [(XG.name, "o", c) for c in range(KC)], writes=[XS.name], dma="xg")
            S.state[XG.name] = [S.latest["d:%d" % S.lane_map["xg"]], {}]
        barrier(S)


def setup_common(P):
    S, nc = P.S, P.nc
    P.wctr = 0
    PS = [P.ps("PS%d" % i) for i in range(8)]
    P.stage_f = [P.sb("wsf%d" % i, [128, WCH], F32) for i in range(2)]
    P.stage_b = [P.sb("wsb%d" % i, [128, WCH], BF16) for i in range(2)]
    B = {}
    B["XG"] = P.sb("XG", [128, KC, 512], F32)
    B["HT"] = P.sb("HT", [128, KC, 512], BF16)
    B["SQ"] = [P.sb("SQ%d" % i, [128, 512], F32) for i in range(2)]
    for k in ("MEAN", "VAR", "RSTD", "NMR"):
        B[k] = P.sb(k, [128, 512], F32)
    B["ONES"] = P.sb("ONES", [128, 128], F32)
    B["ONESB"] = P.sb("ONESB", [128, 128], BF16)
    B["LNG"] = P.sb("LNG", [128, 12, 16], F32)
    B["LNB"] = P.sb("LNB", [128, 12, 16], F32)
    lng = P.ext_in("lng", [128, 12, 16])
    lnb = P.ext_in("lnb", [128, 12, 16])
    S.op("sp", lambda e: e.dma_start(out=B["LNG"][:], in_=lng[:]), writes=["LNG"], dma="c0")
    S.op("sp", lambda e: e.dma_start(out=B["LNB"][:], in_=lnb[:]), writes=["LNB"], dma="c1")
    S.op("pool", lambda e: e.memset(B["ONES"][:], 1.0), writes=["ONES"])
    S.op("pool", lambda e: e.memset(B["ONESB"][:], 1.0), writes=["ONESB"])
    barrier(S)
    return PS, B


def common_inputs(inp, r):
    return {"ccol": np.ascontiguousarray(np.stack([col_layout(inp["c"][0]), col_layout(inp["c_ctx"])], axis=-1)),
            "wada": lay_w_ada(inp["w_ada"], r), "bada": lay_b_ada(inp["b_ada"], r),
            "lng": col_layout(inp["ln_g"]).reshape(128, 12, 16), "lnb": col_layout(inp["ln_b"]).reshape(128, 12, 16)}


NT_ALL = CTX + SEQ
EV_OFF = dict(gq=0, gk=512, gv=1024, gr=2048, gdf=3072, gdb=3088, rq=3104, rk=3616, rv=4128, rg=5152)


def lay_even_core(inp, e, r):
    w = inp["ev_w_in"][e]
    hh = r % 4
    if r < 4:
        q0, k0, v0, g0 = EV_OFF["gq"] + hh * 128, EV_OFF["gk"] + hh * 128, EV_OFF["gv"] + hh * 256, EV_OFF["gr"] + hh * 256
    else:
        q0, k0, v0, g0 = EV_OFF["rq"] + hh * 128, EV_OFF["rk"] + hh * 128, EV_OFF["rv"] + hh * 256, EV_OFF["rg"] + hh * 256
    perm = np.concatenate([np.arange(64, 128), np.arange(0, 64)])
    q = w[:, q0:q0 + 128]
    k = w[:, k0:k0 + 128]
    gdf = np.zeros((D, 128), np.float32)
    gdb = np.zeros((D, 128), np.float32)
    wg = np.zeros((32, 2, 128), np.float32)
    if r < 4:
        gdf[:, 0:16] = w[:, EV_OFF["gdf"]:EV_OFF["gdf"] + 16]
        gdb[:, 0:16] = w[:, EV_OFF["gdb"]:EV_OFF["gdb"] + 16]
        wg[0:16, 0] = inp["ev_gla_wg_f"][e][:, hh * 128:(hh + 1) * 128]
        wg[16, 0] = inp["ev_gla_bg_f"][e][hh * 128:(hh + 1) * 128]
        wg[0:16, 1] = inp["ev_gla_wg_b"][e][:, hh * 128:(hh + 1) * 128]
        wg[16, 1] = inp["ev_gla_bg_b"][e][hh * 128:(hh + 1) * 128]
        nwe = np.broadcast_to(inp["ev_gla_norm"][e][None, :], (128, 256))
    else:
        nwe = np.ones((128, 256), np.float32)
    fm = np.stack([q, q[:, perm], k, k[:, perm], gdf, gdb], 0)
    ewf = np.ascontiguousarray(fm.reshape(6, KC, 128, 128).transpose(2, 0, 1, 3)).reshape(128, 6 * KC * 128)
    tm = np.concatenate([w[:, v0:v0 + 256], w[:, g0:g0 + 256]], 1)
    ewt = np.ascontiguousarray(tm.reshape(KC, 128, 512).transpose(1, 0, 2)).reshape(128, KC * 512)
    return ewf, ewt, np.ascontiguousarray(wg), np.ascontiguousarray(nwe, dtype=np.float32)


def even_consts(r):
    hh = r % 4
    ec = np.zeros((128, 6), np.float32)
    ec[0:64, 4] = 1.0
    ec[64:128, 5] = 1.0
    if r < 4:
        ec[:, 0] = -1.0 / 16.0
        ec[:, 2] = -1.0 / 16.0
    else:
        ec[:, 1] = np.log1p(-np.exp2(np.float32(-5.0 - hh))).astype(np.float32)
        ec[:, 3] = np.log1p(-np.exp2(np.float32(-5.5 - hh))).astype(np.float32)
    cosT = np.ones((128, NT_ALL), np.float32)
    sinT = np.zeros((128, NT_ALL), np.float32)
    if r >= 4:
        inv = (np.float32(10000.0) ** (-(np.arange(0, 128, 2, dtype=np.float32)) / np.float32(128))).astype(np.float32)
        ang = np.arange(SEQ, dtype=np.float32)[:, None] * inv[None, :]
        ang = np.concatenate([ang, ang], -1)
        sign = np.concatenate([-np.ones(64), np.ones(64)]).astype(np.float32)
        cosT[:, CTX:] = np.cos(ang).astype(np.float32).T
        sinT[:, CTX:] = (np.sin(ang).astype(np.float32) * sign[None, :]).T
    j = np.arange(128)[:, None]
    i = np.arange(128)[None, :]
    same = (j // 64) == (i // 64)
    mats = np.stack([(same & (j <= i)), (same & (j > i)), (same & (j >= i)), (same & (j < i)),
                     (j == i)], 0).astype(np.float32)
    return ec, cosT, sinT, np.ascontiguousarray(mats.transpose(1, 0, 2))


def build_even(P, PS, B, l, wout, XS, EI):
    S, nc = P.S, P.nc
    mod = P.mod
    XG, HT = B["XG"], B["HT"]
    sfx = "_%d" % l
    HT_loc = P.dram("ht_loc" + sfx, [KC * 128, LAT], BF16)
    HT_full = P.dram("ht_full" + sfx, [NCORES * KC * 128, LAT], BF16)
    HT_ctx = P.dram("ht_ctx" + sfx, [KC * 128, CTX], BF16)
    OBd = P.dram("obd" + sfx, [NT_ALL, 256], F32)
    A_loc = P.dram("a_loc" + sfx, [256, NT_ALL], BF16)
    A_full = P.dram("a_full" + sfx, [NCORES * 256, NT_ALL], BF16)
    for (t0, n, row) in lat_groups(True):
        S.op("sp", lambda e, t0=t0, n=n: e.dma_start(
            out=XG[:, :, 0:n], in_=dram_ap(XS, t0, [[NTOK, 128], [128 * NTOK, KC], [1, n]])),
            reads=[XS.name], writes=[XG.name], dma="xg")
        for kc in range(KC):
            S.op("act", lambda e, kc=kc, n=n, row=row: e.activation(
                out=HT[:, kc, 0:n], in_=XG[:, kc, 0:n], func=AF.Identity,
                bias=mod(l, 3, kc, row), scale=mod(l, 4, kc, row)),
                reads=[XG.name, P.MODname], writes=[HT.name])
        if row == 1:
            S.op("sp", lambda e, n=n: e.dma_start(
                out=dram_ap(HT_ctx, 0, [[CTX, 128], [128 * CTX, KC], [1, n]]), in_=HT[:, :, 0:n]),
                reads=[HT.name], writes=[HT_ctx.name], dma="hts")
        else:
            S.op("sp", lambda e, n=n, t0=t0: e.dma_start(
                out=dram_ap(HT_loc, t0 - CTX, [[LAT, 128], [128 * LAT, KC], [1, n]]), in_=HT[:, :, 0:n]),
                reads=[HT.name], writes=[HT_loc.name], dma="hts")
    S.op("pool", lambda e: e.collective_compute("AllGather", ALU.bypass, replica_groups=[list(range(NCORES))],
                                                ins=[HT_loc.ap().opt()], outs=[HT_full.ap().opt()]),
         reads=[HT_loc.name], writes=[HT_full.name], dma="cc", inc=1)
    barrier(S)
    import contextlib
    with contextlib.ExitStack() as es:
        def sbt(name, shape, dt):
            return es.enter_context(nc.sbuf_tensor(name + sfx, list(shape), dt))
        EWF = sbt("EWF", [128, 6, KC * 128], BF16)
        EWT = sbt("EWT", [128, KC, 512], BF16)
        WG = sbt("WG", [32, 2, 128], F32)
        NWE = sbt("NWE", [128, 256], F32)
        EC = sbt("EC", [128, 6], F32)
        MATS = sbt("MATS", [128, 5, 128], F32)
        GD = sbt("GD", [32, 128], F32)
        KHA = [sbt("KHA%d" % i, [128, 128], BF16) for i in range(2)]
        KHB = [sbt("KHB%d" % i, [128, 128], BF16) for i in range(2)]
        QTA = [sbt("QTA%d" % i, [128, 128], BF16) for i in range(2)]
        QTB = [sbt("QTB%d" % i, [128, 128], BF16) for i in range(2)]
        HTt = [sbt("HTt%d" % i, [128, KC, 128], BF16) for i in range(2)]
        CS = [sbt("CS%d" % i, [128, 128], F32) for i in range(2)]
        SN = [sbt("SN%d" % i, [128, 128], F32) for i in range(2)]
        QP = sbt("QP", [128, 128], F32)
        KP = sbt("KP", [128, 128], F32)
        T1 = sbt("T1", [128, 128], F32)
        LG = sbt("LG", [128, 128], F32)
        EBb = [sbt("EB%d" % i, [128, 128], F32) for i in range(2)]
        ENB = sbt("ENB", [128, 128], F32)
        ED = sbt("ED", [128, 128], F32)
        QT = [sbt("QT%d" % i, [128, 128], BF16) for i in range(2)]
        KT = [sbt("KT%d" % i, [128, 128], BF16) for i in range(2)]
        KH = [sbt("KH%d" % i, [128, 128], BF16) for i in range(2)]
        VBt = [sbt("VBt%d" % i, [128, 256], BF16) for i in range(2)]
        SGT = [sbt("SGT%d" % i, [128, 256], F32) for i in range(2)]
        AM = sbt("AM", [128, 128], BF16)
        S32 = [sbt("S32_%d" % i, [128, 256], F32) for i in range(2)]
        SBF = [sbt("SBF%d" % i, [128, 256], BF16) for i in range(4)]
        OS = sbt("OS", [128, 256], F32)
        OBt = [sbt("OBt%d" % i, [128, 256], F32) for i in range(2)]
        JK = sbt("JK", [128, 256], F32)
        SS = sbt("SS", [128, 2], F32)
        AT2 = [sbt("AT2_%d" % i, [128, 2, 128], BF16) for i in range(2)]
        for ci in range(6):
            sf, sbb = P.stage_f[ci % 2], P.stage_b[ci % 2]
            S.op("sp", lambda e, sf=sf, ci=ci: e.dma_start(out=sf[:, 0:2048], in_=EI["ewf"][:, ci * 2048:(ci + 1) * 2048]),
                 writes=[sf.name], dma="wl%d" % (ci % 2))
            S.op("pool", lambda e, sf=sf, ci=ci: e.tensor_copy(out=EWF[:, ci, :], in_=sf[:, 0:2048]),
                 reads=[sf.name], writes=[EWF.name])
        for ci in range(4):
            sf = P.stage_f[ci % 2]
            S.op("sp", lambda e, sf=sf, ci=ci: e.dma_start(out=sf[:, 0:2048], in_=EI["ewt"][:, ci * 2048:(ci + 1) * 2048]),
                 writes=[sf.name], dma="wl%d" % (ci % 2))
            S.op("pool", lambda e, sf=sf, ci=ci: e.tensor_copy(
                out=EWT[:, ci * 4:(ci + 1) * 4, :].rearrange("p a b -> p (a b)"), in_=sf[:, 0:2048]),
                reads=[sf.name], writes=[EWT.name])
        for (dst_, src_, nm) in ((WG, EI["wg"], "e0"), (NWE, EI["nwe"], "e1"), (EC, EI["econst"], "e2"), (MATS, EI["emats"], "e3")):
            S.op("sp", lambda e, dst_=dst_, src_=src_: e.dma_start(out=dst_[:], in_=src_[:]), writes=[dst_.name], dma=nm)
        S.op("pool", lambda e: e.memset(GD[:], 1.0), writes=[GD.name])
        for tq in QTA + QTB:
            S.op("pool", lambda e, tq=tq: e.memset(tq[:], 0.0), writes=[tq.name])
        barrier(S)
        QSCALE = 128 ** -0.5
        tiles_f = [("c", 0), ("c", 1)] + [("l", t) for t in range(SEQ // 128)]
        tiles_b = [("c", 1), ("c", 0)] + [("l", t) for t in range(SEQ // 128 - 1, -1, -1)]
        ctr = {"k": 0, "sb": 0}

        def tok0(tile):
            return tile[1] * 128 if tile[0] == "c" else CTX + tile[1] * 128

        def stage1(tile, d, k):
            ht, cs, sn = HTt[k], CS[k], SN[k]
            g0 = tok0(tile)
            if tile[0] == "c":
                S.op("sp", lambda e: e.dma_start(
                    out=ht[:], in_=dram_ap(HT_ctx, tile[1] * 128, [[CTX, 128], [128 * CTX, KC], [1, 128]])),
                    reads=[HT_ctx.name], writes=[ht.name], dma="eht%d" % k)
            else:
                b_, c_ = tile[1] // 16, tile[1] % 16
                S.op("sp", lambda e: e.dma_start(
                    out=ht[:], in_=dram_ap(HT_full, b_ * KC * 128 * LAT + c_ * 128, [[LAT, 128], [128 * LAT, KC], [1, 128]])),
                    reads=[HT_full.name], writes=[ht.name], dma="eht%d" % k)
            S.op("sp", lambda e: e.dma_start(out=cs[:], in_=EI["ecos"][:, g0:g0 + 128]), writes=[cs.name], dma="ecs%d" % k)
            S.op("sp", lambda e: e.dma_start(out=sn[:], in_=EI["esin"][:, g0:g0 + 128]), writes=[sn.name], dma="esn%d" % k)
            fm, tmb, gb, cb = PS[0], PS[2], PS[1], PS[3]
            for ch in range(4):
                for kc in range(KC):
                    S.op("pe", lambda e, ch=ch, kc=kc: e.matmul(
                        fm[:, ch * 128:(ch + 1) * 128], lhsT=EWF[:, ch, kc * 128:(kc + 1) * 128], rhs=ht[:, kc, :],
                        start=(kc == 0), stop=(kc == KC - 1)),
                        reads=[EWF.name, ht.name], writes=[fm.name])
            for kc in range(KC):
                S.op("pe", lambda e, kc=kc: e.matmul(
                    gb[0:16, 0:128], lhsT=EWF[:, 4 + d, kc * 128:kc * 128 + 16], rhs=ht[:, kc, :],
                    start=(kc == 0), stop=(kc == KC - 1)),
                    reads=[EWF.name, ht.name], writes=[gb.name])
            ntm = 512 if d == 0 else 256
            for kc in range(KC):
                S.op("pe", lambda e, kc=kc: e.matmul(
                    tmb[:, 0:ntm], lhsT=ht[:, kc, :], rhs=EWT[:, kc, 0:ntm], start=(kc == 0), stop=(kc == KC - 1)),
                    reads=[EWT.name, ht.name], writes=[tmb.name])
            for (dst_, a0) in ((QP, 0), (KP, 2)):
                S.op("dve", lambda e, dst_=dst_, a0=a0: e.tensor_tensor(out=dst_[:], in0=fm[:, a0 * 128:(a0 + 1) * 128], in1=cs[:], op=ALU.mult),
                     reads=[fm.name, cs.name], writes=[dst_.name])
                S.op("dve", lambda e, a0=a0: e.tensor_tensor(out=T1[:], in0=fm[:, (a0 + 1) * 128:(a0 + 2) * 128], in1=sn[:], op=ALU.mult),
                     reads=[fm.name, sn.name], writes=[T1.name])
                S.op("dve", lambda e, dst_=dst_: e.tensor_tensor(out=dst_[:], in0=dst_[:], in1=T1[:], op=ALU.add),
                     reads=[dst_.name, T1.name], writes=[dst_.name])
            S.op("act", lambda e: e.activation(out=GD[0:16, :], in_=gb[0:16, 0:128], func=AF.Copy),
                 reads=[gb.name], writes=[GD.name])
            S.op("pe", lambda e: e.matmul(gb[:, 128:256], lhsT=GD[0:17, :], rhs=WG[0:17, d, :], start=True, stop=True),
                 reads=[GD.name, WG.name], writes=[gb.name])
            S.op("act", lambda e: e.activation(out=LG[:], in_=gb[:, 128:256], func=AF.Exp, scale=-1.0),
                 reads=[gb.name], writes=[LG.name])
            S.op("dve", lambda e: e.tensor_scalar_add(out=LG[:], in0=LG[:], scalar1=1.0),
                 reads=[LG.name], writes=[LG.name])
            S.op("act", lambda e: e.activation(out=LG[:], in_=LG[:], func=AF.Ln),
                 reads=[LG.name], writes=[LG.name])
            S.op("dve", lambda e: e.tensor_scalar(out=LG[:], in0=LG[:], scalar1=EC[:, 2 * d:2 * d + 1], scalar2=EC[:, 2 * d + 1:2 * d + 2],
                                                  op0=ALU.mult, op1=ALU.add),
                 reads=[LG.name, EC.name], writes=[LG.name])
            cum, stt = MATS[:, 2 * d, :], MATS[:, 2 * d + 1, :]
            S.op("pe", lambda e: e.matmul(cb[:, 0:128], lhsT=LG[:], rhs=cum, start=True, stop=True),
                 reads=[LG.name, MATS.name], writes=[cb.name])
            S.op("pe", lambda e: e.matmul(cb[:, 128:256], lhsT=stt, rhs=LG[:], start=True, stop=True),
                 reads=[LG.name, MATS.name], writes=[cb.name])
            eb = EBb[k]
            S.op("act", lambda e: e.activation(out=eb[:], in_=cb[:, 0:128], func=AF.Exp), reads=[cb.name], writes=[eb.name])
            S.op("act", lambda e: e.activation(out=ENB[:], in_=cb[:, 0:128], func=AF.Exp, scale=-1.0), reads=[cb.name], writes=[ENB.name])
            S.op("act", lambda e: e.activation(out=ED[:], in_=cb[:, 128:256], func=AF.Exp), reads=[cb.name], writes=[ED.name])
            S.op("dve", lambda e: e.scalar_tensor_tensor(out=QT[k][:], in0=QP[:], scalar=QSCALE, in1=eb[:], op0=ALU.mult, op1=ALU.mult),
                 reads=[QP.name, eb.name], writes=[QT[k].name])
            S.op("dve", lambda e: e.tensor_tensor(out=KT[k][:], in0=KP[:], in1=ENB[:], op=ALU.mult),
                 reads=[KP.name, ENB.name], writes=[KT[k].name])
            S.op("pool", lambda e: e.tensor_copy(out=QTA[k][:, 0:64], in_=QT[k][:, 0:64]), reads=[QT[k].name], writes=[QTA[k].name])
            S.op("pool", lambda e: e.tensor_copy(out=QTB[k][:, 64:128], in_=QT[k][:, 64:128]), reads=[QT[k].name], writes=[QTB[k].name])
            S.op("pe", lambda e: e.transpose(gb[:, 256:384], KP[:], MATS[:, 4, :]),
                 reads=[KP.name, MATS.name], writes=[gb.name])
            S.op("dve", lambda e: e.tensor_tensor(out=KH[k][:], in0=gb[:, 256:384], in1=ED[:], op=ALU.mult),
                 reads=[gb.name, ED.name], writes=[KH[k].name])
            S.op("pool", lambda e: e.tensor_scalar_mul(out=KHA[k][:], in0=KH[k][:], scalar1=EC[:, 4:5]), reads=[KH[k].name, EC.name], writes=[KHA[k].name])
            S.op("pool", lambda e: e.tensor_scalar_mul(out=KHB[k][:], in0=KH[k][:], scalar1=EC[:, 5:6]), reads=[KH[k].name, EC.name], writes=[KHB[k].name])
            S.op("act", lambda e: e.activation(out=VBt[k][:], in_=tmb[:, 0:256], func=AF.Copy), reads=[tmb.name], writes=[VBt[k].name])
            if d == 0:
                S.op("act", lambda e: e.activation(out=SGT[k][:], in_=tmb[:, 256:512], func=AF.Silu), reads=[tmb.name], writes=[SGT[k].name])
                S.op("sp", lambda e: e.dma_start(out=OBt[k][:], in_=OBd[g0:g0 + 128, :]), reads=[OBd.name], writes=[OBt[k].name], dma="eob%d" % k)

        def stage2(tile, d, k, st):
            g0 = tok0(tile)
            atb, ub, ob, trb = PS[6], PS[4], PS[5], PS[7]
            eb = EBb[k]
            S.op("pe", lambda e: e.matmul(atb[:, 0:128], lhsT=KT[k][:], rhs=QT[k][:], start=True, stop=True),
                 reads=[KT[k].name, QT[k].name], writes=[atb.name])
            S.op("dve", lambda e: e.tensor_tensor(out=AM[:], in0=atb[:, 0:128], in1=MATS[:, 2 * d, :], op=ALU.mult),
                 reads=[atb.name, MATS.name], writes=[AM.name])
            order = [(0, 63), (64, 127)] if d == 0 else [(64, 64), (0, 0)]
            for ui, (x0, _) in enumerate(order):
                khx = KHA[k] if x0 == 0 else KHB[k]
                S.op("pe", lambda e, ui=ui, khx=khx: e.matmul(
                    ub[:, ui * 256:(ui + 1) * 256], lhsT=khx[:], rhs=VBt[k][:], start=True, stop=True),
                    reads=[khx.name, VBt[k].name], writes=[ub.name])
            S.op("pe", lambda e: e.matmul(ob[:, 0:256], lhsT=AM[:], rhs=VBt[k][:], start=True, stop=False),
                 reads=[AM.name, VBt[k].name], writes=[ob.name])
            for ui, (x0, xe) in enumerate(order):
                sb_cur = st["sb"]
                qtx = QTA[k] if x0 == 0 else QTB[k]
                S.op("pe", lambda e, qtx=qtx, sb_cur=sb_cur, ui=ui: e.matmul(
                    ob[:, 0:256], lhsT=qtx[:], rhs=sb_cur[:], start=False, stop=(ui == 1)),
                    reads=[qtx.name, sb_cur.name], writes=[ob.name])
                s_old, s_new = st["s32"], S32[(st["si"] + 1) % 2]
                st["si"] += 1
                S.op("dve", lambda e, s_old=s_old, s_new=s_new, xe=xe, ui=ui: e.scalar_tensor_tensor(
                    out=s_new[:], in0=s_old[:], scalar=eb[:, xe:xe + 1], in1=ub[:, ui * 256:(ui + 1) * 256], op0=ALU.mult, op1=ALU.add),
                    reads=[s_old.name, eb.name, ub.name], writes=[s_new.name])
                sb_new = SBF[ctr["sb"] % 4]
                ctr["sb"] += 1
                S.op("pool", lambda e, s_new=s_new, sb_new=sb_new: e.tensor_copy(out=sb_new[:], in_=s_new[:]),
                     reads=[s_new.name], writes=[sb_new.name])
                st["s32"], st["sb"] = s_new, sb_new
            if d == 1:
                S.op("act", lambda e: e.activation(out=OS[:], in_=ob[:, 0:256], func=AF.Copy), reads=[ob.name], writes=[OS.name])
                S.op("sp", lambda e: e.dma_start(out=OBd[g0:g0 + 128, :], in_=OS[:]), reads=[OS.name], writes=[OBd.name], dma="eos")
            else:
                S.op("dve", lambda e: e.tensor_tensor(out=OS[:], in0=ob[:, 0:256], in1=OBt[k][:], op=ALU.add),
                     reads=[ob.name, OBt[k].name], writes=[OS.name])
                S.op("act", lambda e: e.activation(out=JK[:], in_=OS[:], func=AF.Square),
                     reads=[OS.name], writes=[JK.name])
                S.op("dve", lambda e: e.reduce_sum(out=SS[:, 0:1], in_=JK[:], axis=mybir.AxisListType.X),
                     reads=[JK.name], writes=[SS.name])
                S.op("dve", lambda e: e.tensor_scalar(out=SS[:, 1:2], in0=SS[:, 0:1], scalar1=1.0 / 256, scalar2=RMS_EPS, op0=ALU.mult, op1=ALU.add),
                     reads=[SS.name], writes=[SS.name])
                S.op("act", lambda e: e.activation(out=SS[:, 1:2], in_=SS[:, 1:2], func=AF.Sqrt), reads=[SS.name], writes=[SS.name])
                S.op("dve", lambda e: e.reciprocal(out=SS[:, 1:2], in_=SS[:, 1:2]), reads=[SS.name], writes=[SS.name])
                S.op("dve", lambda e: e.scalar_tensor_tensor(out=OS[:], in0=OS[:], scalar=SS[:, 1:2], in1=NWE[:], op0=ALU.mult, op1=ALU.mult),
                     reads=[OS.name, SS.name, NWE.name], writes=[OS.name])
                S.op("dve", lambda e: e.tensor_tensor(out=OS[:], in0=OS[:], in1=SGT[k][:], op=ALU.mult),
                     reads=[OS.name, SGT[k].name], writes=[OS.name])
                at2 = AT2[k]
                for ec_ in range(2):
                    S.op("pe", lambda e, ec_=ec_: e.transpose(trb[:, ec_ * 128:(ec_ + 1) * 128], OS[:, ec_ * 128:(ec_ + 1) * 128], MATS[:, 4, :]),
                         reads=[OS.name, MATS.name], writes=[trb.name])
                    S.op("act", lambda e, ec_=ec_: e.activation(out=at2[:, ec_, :], in_=trb[:, ec_ * 128:(ec_ + 1) * 128], func=AF.Copy),
                         reads=[trb.name], writes=[at2.name])
                S.op("sp", lambda e: e.dma_start(
                    out=dram_ap(A_loc, g0, [[NT_ALL, 128], [128 * NT_ALL, 2], [1, 128]]), in_=at2[:]),
                    reads=[at2.name], writes=[A_loc.name], dma="eat%d" % k)

        for d, tiles in ((1, tiles_b), (0, tiles_f)):
            st = {"s32": S32[0], "sb": SBF[ctr["sb"] % 4], "si": 0}
            ctr["sb"] += 1
            S.op("dve", lambda e, a=st["s32"]: e.memset(a[:], 0.0), writes=[st["s32"].name])
            S.op("pool", lambda e, a=st["sb"]: e.memset(a[:], 0.0), writes=[st["sb"].name])
            ks = []
            for ti in range(len(tiles) + 1):
                if ti < len(tiles):
                    k = ctr["k"] % 2
                    ctr["k"] += 1
                    ks.append(k)
                    stage1(tiles[ti], d, k)
                if ti >= 1:
                    stage2(tiles[ti - 1], d, ks[ti - 1], st)
        barrier(S)
    S.op("pool", lambda e: e.collective_compute("AllGather", ALU.bypass, replica_groups=[list(range(NCORES))],
                                                ins=[A_loc.ap().opt()], outs=[A_full.ap().opt()]),
         reads=[A_loc.name], writes=[A_full.name], dma="cc", inc=1)

    def loader(e, OTG, t0, n):
        if t0 < CTX:
            off = t0
        else:
            off = e.partition_id() * LAT + t0
        return e.dma_start(out=OTG[:, :, 0:n], in_=bass.AP(A_full, off, [[NT_ALL, 128], [128 * NT_ALL, 16], [1, n]]))
    build_outproj(P, PS, B, l, wout, XS, True, loader, A_full.name)


def lay_wout16(w):
    return np.ascontiguousarray(w.reshape(16, 128, 16, 128).transpose(2, 1, 0, 3))


def even_ext_inputs(P, e):
    EI = {"ewf": P.ext_in("ewf%d" % e, [128, 6 * KC * 128]), "ewt": P.ext_in("ewt%d" % e, [128, KC * 512]),
          "wg": P.ext_in("ewg%d" % e, [32, 2, 128]), "nwe": P.ext_in("enwe%d" % e, [128, 256])}
    return EI


def even_shared_inputs(P):
    return {"econst": P.ext_in("econst", [128, 6]), "ecos": P.ext_in("ecos", [128, NT_ALL]),
            "esin": P.ext_in("esin", [128, NT_ALL]), "emats": P.ext_in("emats", [128, 5, 128])}


def build_full(inp, depth=DEPTH):
    import contextlib
    P = Prog()
    S, nc = P.S, P.nc
    PS, B = setup_common(P)
    build_ada(P, PS)
    xin = P.ext_in("xin", [KC, 128, NTOK])
    yout = P.ext_out("yout", [KC, 128, LAT])
    XS = P.dram("XS", [KC, 128, NTOK], F32)
    cosT = P.ext_in("cosT", [64, NTOK])
    sinT = P.ext_in("sinT", [64, NTOK])
    nw = P.ext_in("mla_nw", [2, 128, 2, 4])
    ESH = even_shared_inputs(P)
    per_core = [dict() for _ in range(NCORES)]

    def reg(name, arr):
        sh, m = shard_flat(arr)
        for r in range(NCORES):
            per_core[r][name] = sh[r]
        return m

    phases = []
    for l in range(depth):
        phases.append(("ffn", l, 0))
        phases.append(("mix", l))
        phases.append(("ffn", l, 1))
    Wd = {}

    def pipe(ph):
        if ph[0] == "ffn":
            _, l, f = ph
            mi = reg("wi%d%d" % (l, f), lay_w_in(inp["w_ffn_in"][l, f]))
            mo = reg("wo%d%d" % (l, f), lay_w_out(inp["w_ffn_out"][l, f]))
            Wd[ph] = (weight_pipeline(P, "wi%d%d" % (l, f), mi, P.stage_f, P.stage_b),
                      weight_pipeline(P, "wo%d%d" % (l, f), mo, P.stage_f, P.stage_b))
        else:
            l = ph[1]
            if l % 2 == 0:
                e = l // 2
                m = reg("evwo%d" % e, lay_wout16(inp["ev_w_out"][e]))
                Wd[ph] = weight_pipeline(P, "evwo%d" % e, m, P.stage_f, P.stage_b)
            else:
                o = l // 2
                mw = lay_mla_weights(inp["od_w_in"][o], inp["od_w_uq"][o], inp["od_w_ukv"][o], inp["od_w_out"][o])
                W = {}
                for k, a in mw.items():
                    m = reg("mw%d_%s" % (o, k), a)
                    W[k] = weight_pipeline(P, "mw%d_%s" % (o, k), m, P.stage_f, P.stage_b)
                Wd[ph] = W

    with nc.allow_low_precision("bf16 matmul operands, fp32 accumulation"):
        pipe(phases[0])
        for pi, ph in enumerate(phases):
            if pi + 1 < len(phases):
                pipe(phases[pi + 1])
            last = (pi == len(phases) - 1)
            if ph[0] == "ffn":
                _, l, f = ph
                need_ctx = not (l == DEPTH - 1 and f == 1)
                src = xin if pi == 0 else XS
                with contextlib.ExitStack() as es:
                    def sbt(name, shape, dt):
                        return es.enter_context(nc.sbuf_tensor("%s_p%d" % (name, pi), list(shape), dt))
                    Bf = dict(B)
                    Bf["AT"] = sbt("AT", [128, NJ, 512], BF16)
                    Bf["WJ"] = [sbt("WJ%d" % i, [128, KC * 256], BF16) for i in range(3)]
                    Bf["WO"] = [sbt("WO%d" % i, [128, NJ * 128], BF16) for i in range(2)]
                    Bf["SG"] = [sbt("SG%d" % i, [128, 512], F32) for i in range(2)]
                    wi_full, wo_full = Wd[ph]
                    if last:
                        build_ffn(P, PS, Bf, l, f, wi_full, wo_full, src, yout, CTX, lat_groups(False))
                    else:
                        build_ffn(P, PS, Bf, l, f, wi_full, wo_full, src, XS, 0, lat_groups(need_ctx))
                    barrier(S)
            else:
                l = ph[1]
                need_ctx = l < DEPTH - 1
                if l % 2 == 0:
                    e = l // 2
                    EI = even_ext_inputs(P, e)
                    EI.update(ESH)
                    build_even(P, PS, B, l, Wd[ph], XS, EI)
                else:
                    T = build_mla(P, PS, B, l, Wd[ph], XS, need_ctx, (cosT, sinT, nw))
                    build_mla_attn(P, PS, B, l, Wd[ph], XS, need_ctx, T)
        S.emit()
    ctxT = inp["ctx"][0].T.reshape(KC, 128, CTX)
    nwa = np.stack([np.stack([inp["od_q_norm"][o].reshape(4, 128).T, inp["od_kv_norm"][o].reshape(4, 128).T], 1)
                    for o in range(2)], 0).astype(np.float32)
    for r in range(NCORES):
        d = per_core[r]
        d.update(common_inputs(inp, r))
        xT = inp["x"][0, r * LAT:(r + 1) * LAT].T.reshape(KC, 128, LAT)
        d["xin"] = np.ascontiguousarray(np.concatenate([ctxT, xT], axis=2))
        c_, s_ = axial_tables_np(r)
        d["cosT"], d["sinT"], d["mla_nw"] = c_, s_, np.ascontiguousarray(nwa)
        ec, ecos, esin, mats = even_consts(r)
        d["econst"], d["ecos"], d["esin"], d["emats"] = ec, ecos, esin, mats
        for e in range((depth + 1) // 2):
            ewf, ewt, wg, nwe = lay_even_core(inp, e, r)
            d["ewf%d" % e], d["ewt%d" % e], d["ewg%d" % e], d["enwe%d" % e] = ewf, ewt, wg, nwe
    return P, per_core


def kernel(**inputs):
    inp = {k: np.asarray(v) for k, v in inputs.items()}
    P, per_core = build_full(inp)
    names = set()
    for alloc in P.nc.allocations:
        pass
    res = run_bass_kernel_spmd(P.nc, per_core, core_ids=list(range(NCORES)))
    out = np.empty((1, SEQ, D), np.float32)
    for r in range(NCORES):
        y = np.asarray(res.results[r]["yout"]).reshape(D, LAT)
        out[0, r * LAT:(r + 1) * LAT, :] = y.T
    return out
```
